# Optimizing a Trainium2 kernel written in Bass

```python
import math
import jax, jax.numpy as jnp
from jax import lax
import numpy as np

D_MODEL = 1024
BATCH = 8
SEQ = 2048
DEPTH = 1
DEC_BATCH = 128
DEC_SEQ = 1
PAST_LEN = 16384
PAGE_SIZE = 128

CHUNK = 128
GMLP_WIDTH = D_MODEL
GMLP_GROUPS = 8
GMLP_GROUP_CH = GMLP_WIDTH // GMLP_GROUPS
SSM_GROUP_CH = 16
SSM_WIDTH = D_MODEL // 2
SSM_GROUPS = SSM_WIDTH // SSM_GROUP_CH
SSM_STATE = 64
IN_WIDTH = 2 * GMLP_WIDTH + SSM_WIDTH + 2 * D_MODEL
D_FF = int(math.ceil(8 * D_MODEL / 3 / 256) * 256)
EPS = 1e-6

kernel_name = "gmlp_s5_gated_hybrid_step"


def rmsnorm(x, g):
    xf = x.astype(jnp.float32)
    y = xf * lax.rsqrt(jnp.mean(xf * xf, axis=-1, keepdims=True) + EPS)
    return (y * g.astype(jnp.float32)).astype(x.dtype)


def layernorm(x, g, b):
    xf = x.astype(jnp.float32)
    mu = jnp.mean(xf, axis=-1, keepdims=True)
    var = jnp.mean(jnp.square(xf - mu), axis=-1, keepdims=True)
    y = (xf - mu) * lax.rsqrt(var + EPS)
    return (y * g.astype(jnp.float32) + b.astype(jnp.float32)).astype(x.dtype)


def chunk_spatial_gate(u, v, w_spatial, b_spatial):
    bsz, length, _ = v.shape
    n_chunks = -(-length // CHUNK)
    pad = n_chunks * CHUNK - length
    vp = jnp.pad(v, ((0, 0), (0, pad), (0, 0))).reshape(bsz, n_chunks, CHUNK, GMLP_GROUPS, GMLP_GROUP_CH)
    mask = jnp.tril(jnp.ones((CHUNK, CHUNK), dtype=bool))
    w = jnp.where(mask[None], w_spatial, jnp.zeros_like(w_spatial))
    mixed = jnp.einsum('gts,bnsgc->bntgc', w, vp) + b_spatial.T[None, None, :, :, None]
    mixed = mixed.reshape(bsz, n_chunks * CHUNK, GMLP_WIDTH)[:, :length]
    return u * mixed


def s5_scan(s, h0_re, h0_im, lam_re, lam_im, log_dt, b_re, b_im, c_re, c_im, d_skip):
    f32 = jnp.float32
    bsz, length, _ = s.shape
    u = s.astype(f32).reshape(bsz, length, SSM_GROUPS, SSM_GROUP_CH)
    lam = lax.complex(lam_re.astype(f32), lam_im.astype(f32))
    dt = jnp.exp(log_dt.astype(f32))[:, None]
    lam_bar = jnp.exp(lam * dt)
    b_mat = lax.complex(b_re.astype(f32), b_im.astype(f32))
    b_bar = ((lam_bar - 1.0) / lam)[..., None] * b_mat
    bu = jnp.einsum('gph,blgh->blgp', b_bar, u.astype(jnp.complex64))
    h0 = lax.complex(h0_re.astype(f32), h0_im.astype(f32))
    bu = bu.at[:, 0].add(lam_bar[None] * h0)
    a = jnp.broadcast_to(lam_bar, bu.shape)

    def combine(e1, e2):
        a1, b1 = e1
        a2, b2 = e2
        return a1 * a2, a2 * b1 + b2

    _, h = lax.associative_scan(combine, (a, bu), axis=1)
    c_mat = lax.complex(c_re.astype(f32), c_im.astype(f32))
    y = jnp.einsum('ghp,blgp->blgh', c_mat, h).real \
        + d_skip.astype(f32).reshape(SSM_GROUPS, SSM_GROUP_CH) * u
    h_last = h[:, -1]
    return y.reshape(bsz, length, SSM_WIDTH), jnp.real(h_last), jnp.imag(h_last)


def layer(x, h0_re, h0_im, norm_mix_g, w_in, ln_v_g, ln_v_b, w_spatial, b_spatial,
          lam_re, lam_im, log_dt, b_re, b_im, c_re, c_im, d_skip,
          w_branch_a, w_branch_b, w_out, norm_ffn_g, w_gate_ffn, w_up_ffn, w_down_ffn):
    h = rmsnorm(x, norm_mix_g)
    z = h @ w_in
    u, v, s, ga, gb = jnp.split(
        z, [GMLP_WIDTH, 2 * GMLP_WIDTH, 2 * GMLP_WIDTH + SSM_WIDTH, 2 * GMLP_WIDTH + SSM_WIDTH + D_MODEL], axis=-1)
    u = jax.nn.gelu(u)
    v = layernorm(jax.nn.gelu(v), ln_v_g, ln_v_b)
    ya = chunk_spatial_gate(u, v, w_spatial, b_spatial) @ w_branch_a
    ys, h_re, h_im = s5_scan(s, h0_re, h0_im, lam_re, lam_im, log_dt, b_re, b_im, c_re, c_im, d_skip)
    pb = ys.astype(x.dtype) @ w_branch_b
    yb = pb[..., :D_MODEL] * jax.nn.sigmoid(pb[..., D_MODEL:])
    merged = jax.nn.sigmoid(ga) * ya + jax.nn.sigmoid(gb) * yb
    x = x + merged @ w_out
    h2 = rmsnorm(x, norm_ffn_g)
    x = x + (jax.nn.silu(h2 @ w_gate_ffn) * (h2 @ w_up_ffn)) @ w_down_ffn
    return x, h_re, h_im, v


def setup_inputs(seed: int = 0) -> dict:
    key = jax.random.key(seed)
    ks = jax.random.split(key, 32)
    f32 = jnp.float32
    nrm = lambda k, shape, scale: jax.random.normal(k, shape, f32) * scale
    n_idx = jnp.arange(SSM_STATE, dtype=f32)
    lam_re = -0.5 + nrm(ks[0], (DEPTH, SSM_GROUPS, SSM_STATE), 0.01)
    lam_im = math.pi * n_idx[None, None, :] + nrm(ks[1], (DEPTH, SSM_GROUPS, SSM_STATE), 0.01)
    log_dt = jax.random.uniform(ks[2], (DEPTH, SSM_GROUPS), f32, math.log(1e-3), math.log(1e-1))
    return {
        "x_prompt": nrm(ks[3], (BATCH, SEQ, D_MODEL), 1.0),
        "x_sample": nrm(ks[4], (DEC_BATCH, DEC_SEQ, D_MODEL), 1.0),
        "state_ssm_re": nrm(ks[5], (DEPTH, DEC_BATCH, SSM_GROUPS, SSM_STATE), 1.0),
        "state_ssm_im": nrm(ks[6], (DEPTH, DEC_BATCH, SSM_GROUPS, SSM_STATE), 1.0),
        "norm_mix_g": 1.0 + nrm(ks[7], (DEPTH, D_MODEL), 0.02),
        "w_in": nrm(ks[8], (DEPTH, D_MODEL, IN_WIDTH), D_MODEL ** -0.5),
        "ln_v_g": 1.0 + nrm(ks[9], (DEPTH, GMLP_WIDTH), 0.02),
        "ln_v_b": nrm(ks[10], (DEPTH, GMLP_WIDTH), 0.02),
        "w_spatial": nrm(ks[11], (DEPTH, GMLP_GROUPS, CHUNK, CHUNK), CHUNK ** -0.5),
        "b_spatial": 1.0 + nrm(ks[12], (DEPTH, GMLP_GROUPS, CHUNK), 0.02),
        "ssm_lam_re": lam_re,
        "ssm_lam_im": lam_im,
        "ssm_log_dt": log_dt,
        "ssm_b_re": nrm(ks[13], (DEPTH, SSM_GROUPS, SSM_STATE, SSM_GROUP_CH), (2 * SSM_GROUP_CH) ** -0.5),
        "ssm_b_im": nrm(ks[14], (DEPTH, SSM_GROUPS, SSM_STATE, SSM_GROUP_CH), (2 * SSM_GROUP_CH) ** -0.5),
        "ssm_c_re": nrm(ks[15], (DEPTH, SSM_GROUPS, SSM_GROUP_CH, SSM_STATE), (2 * SSM_STATE) ** -0.5),
        "ssm_c_im": nrm(ks[16], (DEPTH, SSM_GROUPS, SSM_GROUP_CH, SSM_STATE), (2 * SSM_STATE) ** -0.5),
        "ssm_d": nrm(ks[17], (DEPTH, SSM_WIDTH), 1.0),
        "w_branch_a": nrm(ks[18], (DEPTH, GMLP_WIDTH, D_MODEL), GMLP_WIDTH ** -0.5),
        "w_branch_b": nrm(ks[19], (DEPTH, SSM_WIDTH, 2 * D_MODEL), SSM_WIDTH ** -0.5),
        "w_out": nrm(ks[20], (DEPTH, D_MODEL, D_MODEL), D_MODEL ** -0.5),
        "norm_ffn_g": 1.0 + nrm(ks[21], (DEPTH, D_MODEL), 0.02),
        "w_gate_ffn": nrm(ks[22], (DEPTH, D_MODEL, D_FF), D_MODEL ** -0.5),
        "w_up_ffn": nrm(ks[23], (DEPTH, D_MODEL, D_FF), D_MODEL ** -0.5),
        "w_down_ffn": nrm(ks[24], (DEPTH, D_FF, D_MODEL), D_FF ** -0.5),
        "norm_final_g": 1.0 + nrm(ks[25], (D_MODEL,), 0.02),
    }


def reference(x_prompt, x_sample, state_ssm_re, state_ssm_im, norm_mix_g, w_in, ln_v_g, ln_v_b,
              w_spatial, b_spatial, ssm_lam_re, ssm_lam_im, ssm_log_dt, ssm_b_re, ssm_b_im,
              ssm_c_re, ssm_c_im, ssm_d, w_branch_a, w_branch_b, w_out, norm_ffn_g,
              w_gate_ffn, w_up_ffn, w_down_ffn, norm_final_g):
    xp, xs = x_prompt, x_sample
    zeros_state = jnp.zeros((x_prompt.shape[0], SSM_GROUPS, SSM_STATE), jnp.float32)
    p_re, p_im, s_re, s_im, s_v = [], [], [], [], []
    for l in range(DEPTH):
        params = (norm_mix_g[l], w_in[l], ln_v_g[l], ln_v_b[l], w_spatial[l], b_spatial[l],
                  ssm_lam_re[l], ssm_lam_im[l], ssm_log_dt[l], ssm_b_re[l], ssm_b_im[l],
                  ssm_c_re[l], ssm_c_im[l], ssm_d[l], w_branch_a[l], w_branch_b[l], w_out[l],
                  norm_ffn_g[l], w_gate_ffn[l], w_up_ffn[l], w_down_ffn[l])
        xp, hr, hi, _ = layer(xp, zeros_state, zeros_state, *params)
        p_re.append(hr)
        p_im.append(hi)
        xs, hr, hi, v_rows = layer(xs, state_ssm_re[l], state_ssm_im[l], *params)
        s_re.append(hr)
        s_im.append(hi)
        s_v.append(v_rows)
    y_prompt = rmsnorm(xp, norm_final_g)
    y_sample = rmsnorm(xs, norm_final_g)
    new_ssm_re_prompt = jnp.stack(p_re)
    new_ssm_im_prompt = jnp.stack(p_im)
    new_ssm_re_sample = jnp.stack(s_re)
    new_ssm_im_sample = jnp.stack(s_im)
    new_chunk_v_sample = jnp.stack(s_v)
    return (y_prompt, y_sample, new_ssm_re_prompt, new_ssm_im_prompt, new_ssm_re_sample, new_ssm_im_sample, new_chunk_v_sample)
```

```python
import contextlib
import numpy as np
import concourse.bass as bass
import concourse.mybir as mybir
from concourse.bass_utils import run_bass_kernel_spmd

F32 = mybir.dt.float32
BF16 = mybir.dt.bfloat16
AF = mybir.ActivationFunctionType
ALU = mybir.AluOpType

D = 1024
SEQ = 2048
NSAMP = 16
DFF = 2816
EPS = 1e-6
NSLOT = 5
NPASS = 5


class Sched:
    ENGS = ["pe", "act", "dve", "pool", "sp"]

    def __init__(self, nc):
        self.nc = nc
        self.ops = []
        self.last_w = {}
        self.readers = {}

    def add(self, eng, fn, reads=(), writes=(), dkey=None):
        i = len(self.ops)
        deps = set()
        for k in list(reads) + list(writes):
            if k in self.last_w:
                deps.add(self.last_w[k])
        for k in writes:
            deps.update(self.readers.get(k, ()))
        self.ops.append(dict(eng=eng, fn=fn, deps=deps, dkey=dkey, sig=False))
        for k in reads:
            self.readers.setdefault(k, []).append(i)
        for k in writes:
            self.last_w[k] = i
            self.readers[k] = []
        return i

    def emit(self, final_wait_eng="sp"):
        nc = self.nc
        ops = self.ops
        for op in ops:
            for d in op["deps"]:
                dop = ops[d]
                if dop["dkey"] is not None:
                    continue
                if dop["eng"] == op["eng"] and dop["eng"] == "pe":
                    continue
                dop["sig"] = True
        eng_cnt = {e: 0 for e in self.ENGS}
        dma_cnt = {}
        for op in ops:
            if op["dkey"] is not None:
                dma_cnt[op["dkey"]] = dma_cnt.get(op["dkey"], 0) + 1
                op["tick"] = 16 * dma_cnt[op["dkey"]]
            elif op["sig"]:
                eng_cnt[op["eng"]] += 1
                op["tick"] = eng_cnt[op["eng"]]
        with contextlib.ExitStack() as st:
            esem = {e: st.enter_context(nc.semaphore("s_" + e)) for e in ["pe", "act", "dve", "pool"]}
            dsem = {k: st.enter_context(nc.semaphore("d_%d" % n)) for n, k in enumerate(dma_cnt)}
            block = st.enter_context(nc.Block())

            def stream(ename):
                def body(eng):
                    seen = {}
                    for op in ops:
                        if op["eng"] != ename:
                            continue
                        need = {}
                        for d in op["deps"]:
                            dop = ops[d]
                            if dop["dkey"] is not None:
                                s = dsem[dop["dkey"]]
                            else:
                                if not dop["sig"]:
                                    continue
                                if dop["eng"] == ename and ename == "pe":
                                    continue
                                s = esem[dop["eng"]]
                            key = id(s)
                            if dop["tick"] > need.get(key, (None, 0))[1]:
                                need[key] = (s, dop["tick"])
                        for key, (s, v) in need.items():
                            if v > seen.get(key, 0):
                                eng.wait_ge(s, v)
                                seen[key] = v
                        ins = op["fn"](eng)
                        if op["dkey"] is not None:
                            ins.then_inc(dsem[op["dkey"]], 16)
                        elif op["sig"]:
                            ins.then_inc(esem[ename], 1)
                    if ename == final_wait_eng:
                        for k, c in dma_cnt.items():
                            eng.wait_ge(dsem[k], 16 * c)

                return body

            block.tensor(stream("pe"))
            block.scalar(stream("act"))
            block.vector(stream("dve"))
            block.gpsimd(stream("pool"))
            block.sync(stream("sp"))


def build_program():
    nc = bass.Bass("TRN2", target_bir_lowering=False)

    def din(name, shape):
        return nc.dram_tensor(name, list(shape), F32, kind="ExternalInput").ap()

    def dout(name, shape):
        return nc.dram_tensor(name, list(shape), F32, kind="ExternalOutput").ap()

    xp = din("xp", [SEQ, D])
    xs = din("xs", [NSAMP, D])
    h0re = din("h0re", [NSAMP, 2048])
    h0im = din("h0im", [NSAMP, 2048])
    g_mixT = din("g_mixT", [128, 8])
    g_ffnT = din("g_ffnT", [128, 8])
    g_fin = din("g_fin", [D])
    ln_g = din("ln_g", [D])
    ln_b = din("ln_b", [D])
    w_in = din("w_in", [D, 4608])
    w_a = din("w_a", [D, D])
    w_b = din("w_b", [512, 2048])
    w_out = din("w_out", [D, D])
    w_g = din("w_g", [D, DFF])
    w_u = din("w_u", [D, DFF])
    w_d = din("w_d", [DFF, D])
    wspT = din("wspT", [128, 8, 128])
    trimask = din("trimask", [128, 128])
    bsp = din("bsp", [1, 1024])
    wdiag = din("wdiag", [16, 8])
    bsp_s = din("bsp_s", [1, 128])
    ident = din("ident", [128, 128])
    lamre_s = din("lamre_s", [128, 16])
    lamim_s = din("lamim_s", [128, 16])
    logdt_s = din("logdt_s", [128, 16])
    lamre_b = din("lamre_b", [128, 256])
    lamim_b = din("lamim_b", [128, 256])
    logdt_b = din("logdt_b", [128, 256])
    Bre_b = din("Bre_b", [128, 256])
    Bim_b = din("Bim_b", [128, 256])
    maskB = din("maskB", [128, 8])
    Cre_l = din("Cre_l", [128, 256])
    Cim_l = din("Cim_l", [128, 256])
    maskC = din("maskC", [128, 128])
    dvec = din("dvec", [128, 4])

    yp = dout("yp", [SEQ, D])
    ys = dout("ys", [NSAMP, D])
    sre_p = dout("sre_p", [16, 128])
    sim_p = dout("sim_p", [16, 128])
    sre_s = dout("sre_s", [NSAMP, 2048])
    sim_s = dout("sim_s", [NSAMP, 2048])
    v_s = dout("v_s", [NSAMP, D])

    with contextlib.ExitStack() as st:
        def sb(name, shape, dt=F32):
            return st.enter_context(nc.sbuf_tensor(name, list(shape), dt))

        def ps(name, shape, dt=F32):
            return st.enter_context(nc.psum_tensor(name, list(shape), dt))

        ring = sb("ring", [128, NSLOT, 4096], BF16)
        x_sb = [sb("x%d" % i, [128, D]) for i in range(4)]
        h_tm2 = [sb("h_tm%d" % i, [128, D], BF16) for i in range(2)]
        hTa = sb("hTa", [128, 8, 512], BF16)
        hTb = sb("hTb", [128, 8, 512], BF16)
        big = sb("big", [128, 22, 512], BF16)
        vg = sb("vg", [128, D])
        v_ln = [sb("vln%d" % i, [128, D], BF16) for i in range(2)]
        sT = sb("sT", [128, 4, 512], BF16)
        yT = sb("yT", [128, 4, 512], BF16)
        tmp_f = [sb("tmpf%d" % i, [128, 512]) for i in range(2)]
        tmp_b = [sb("tmpb%d" % i, [128, 512], BF16) for i in range(2)]
        stat = [sb("stat%d" % i, [128, 16]) for i in range(4)]
        tSall = sb("tSall", [128, 12, 512])
        tS = [tSall[:, i, :].rearrange("p (a b) -> p a b", a=4) for i in range(12)]
        h_bf = [sb("hbf%d" % i, [128, 2, 4, 128], BF16) for i in range(2)]
        hst = [sb("hst_re", [128, 16]), sb("hst_im", [128, 16])]
        ident_f = sb("ident_f", [128, 128])
        ident_b = sb("ident_b", [128, 128], BF16)
        gmT = sb("gmT", [128, 8])
        gfT = sb("gfT", [128, 8])
        gfin_t = sb("gfin_t", [128, D])
        lng_t = sb("lng_t", [128, D])
        lnb_t = sb("lnb_t", [128, D])
        WspT = sb("WspT", [128, 8, 128], BF16)
        WspT_s = sb("WspT_s", [16, 8, 16], BF16)
        bsp_bf = sb("bsp_bf", [1, 8, 128], BF16)
        bsps_bf = sb("bsps_bf", [1, 8, 16], BF16)
        ones_bf = sb("ones_bf", [1, 128], BF16)
        cosT = sb("cosT", [128, 16, 128])
        sinT = sb("sinT", [128, 16, 128])
        rho_s = sb("rho_s", [128, 16])
        cosB = sb("cosB", [128, 16, 128], BF16)
        sinB = sb("sinB", [128, 16, 128], BF16)
        sst = sb("sst", [128, 2, 16])
        lb_re = sb("lb_re", [128, 16])
        lb_im = sb("lb_im", [128, 16])
        lhsT_BU = sb("lhsT_BU", [128, 4, 4, 2, 128], BF16)
        lhsT_C = sb("lhsT_C", [128, 16, 2, 128], BF16)
        diagD = sb("diagD", [128, 4, 128], BF16)
        h0T = [tSall[:, 6 + i, 0:256].rearrange("p (q t) -> p q t", q=16) for i in range(2)]
        hs = [tSall[:, 8 + i, 0:256].rearrange("p (q t) -> p q t", q=16) for i in range(2)]
        hs_bf = tmp_b[1][:, :].rearrange("p (q a t) -> p q a t", q=16, a=2)
        scrA = vg[:, :].rearrange("p (g t) -> p g t", g=8)
        scrB = tmp_f[1][:, :]
        scrC = x_sb[0][:, 0:640]
        gfin_scr2 = tSall[:, 4:6, :].rearrange("p a b -> p (a b)")
        pT = ps("pT", [128, 8, 128], BF16)
        gbank = [ps("gb%d" % i, [128, 512]) for i in range(4)]
        bRe = ps("bRe", [128, 512])
        bIm = ps("bIm", [128, 512])
        bY = ps("bY", [128, 512])

        S = Sched(nc)
        gb_ctr = [0]

        def next_bank():
            i = gb_ctr[0] % 4
            gb_ctr[0] += 1
            return gbank[i], ("gb", i)

        def ld(dst, src, key, eng="sp"):
            S.add(eng, lambda e: e.dma_start(out=dst, in_=src), writes=[key], dkey=("setup", key))

        ld(ident_f[:], ident, "ident_f")
        ld(ident_b[:], ident, "ident_b", eng="pool")
        ld(gmT[:], g_mixT, "gmT")
        ld(gfT[:], g_ffnT, "gfT")
        ld(gfin_t[:], g_fin.partition_broadcast(128), "gfin_t")
        ld(lng_t[:], ln_g.partition_broadcast(128), "lng_t")
        ld(lnb_t[:], ln_b.partition_broadcast(128), "lnb_t")
        ld(bsp_bf[:], bsp.rearrange("o (g t) -> o g t", g=8), "bsp_bf", eng="pool")
        ld(bsps_bf[:], bsp_s.rearrange("o (g t) -> o g t", g=8), "bsps_bf", eng="pool")
        S.add("dve", lambda e: e.memset(ones_bf[:], 1.0), writes=["ones_bf"])
        S.add("dve", lambda e: e.memset(hst[0][:], 0.0), writes=[("hst", 0, m) for m in range(4)])
        S.add("dve", lambda e: e.memset(hst[1][:], 0.0), writes=[("hst", 1, m) for m in range(4)])

        S.add("sp", lambda e: e.dma_start(out=scrA, in_=wspT), writes=[("vg", 0), ("vg", 1)], dkey=("setup", "scrA"))
        ld(scrB[:, 0:128], trimask, "trimask")
        S.add("dve", lambda e: e.tensor_tensor(out=WspT[:], in0=scrA, in1=scrB[:, 0:128].unsqueeze(1).broadcast_to([128, 8, 128]), op=ALU.mult),
              reads=[("vg", 0), ("vg", 1), "trimask"], writes=["WspT"])
        ld(scrB[:16, 128:136], wdiag, "wdiag")
        S.add("dve", lambda e: e.tensor_tensor(out=WspT_s[:], in0=ident_f[:16, :16].unsqueeze(1).broadcast_to([16, 8, 16]),
                                               in1=scrB[:16, 128:136].unsqueeze(2).broadcast_to([16, 8, 16]), op=ALU.mult),
              reads=["ident_f", "wdiag"], writes=["WspT_s"])
        ld(scrB[:, 136:140], dvec, "dvec")
        for m in range(4):
            S.add("dve", lambda e, m=m: e.tensor_scalar(out=diagD[:, m, :], in0=ident_f[:], scalar1=scrB[:, 136 + m:137 + m],
                                                        scalar2=None, op0=ALU.mult),
                  reads=["ident_f", "dvec"], writes=[("diagD", m)])

        HALF_PI = float(np.pi / 2)

        def lam_tables(pfx, lamre_ap, lamim_ap, logdt_ap, n, bufs, XR=()):
            lr, li, dt, a, c, s, t1, t2 = bufs[:8]
            k = lambda i: "%s_%d" % (pfx, i)
            S.add("sp", lambda e: e.dma_start(out=lr, in_=lamre_ap), reads=list(XR), writes=[k(0)], dkey=("setup", k(0)))
            S.add("sp", lambda e: e.dma_start(out=li, in_=lamim_ap), reads=list(XR), writes=[k(1)], dkey=("setup", k(1)))
            S.add("sp", lambda e: e.dma_start(out=dt, in_=logdt_ap), reads=list(XR), writes=[k(2)], dkey=("setup", k(2)))
            S.add("act", lambda e: e.activation(out=dt, in_=dt, func=AF.Exp), reads=list(XR) + [k(2)], writes=[k(2)])
            S.add("dve", lambda e: e.tensor_tensor(out=a, in0=lr, in1=dt, op=ALU.mult), reads=list(XR) + [k(0), k(2)], writes=[k(3)])
            S.add("act", lambda e: e.activation(out=a, in_=a, func=AF.Exp), reads=list(XR) + [k(3)], writes=[k(3)])
            S.add("dve", lambda e: e.scalar_tensor_tensor(out=t1, in0=li, scalar=1.0 / 16, in1=dt, op0=ALU.mult, op1=ALU.mult),
                  reads=list(XR) + [k(1), k(2)], writes=[k(6)])
            S.add("dve", lambda e: e.tensor_scalar(out=t2, in0=t1, scalar1=HALF_PI, scalar2=None, op0=ALU.add),
                  reads=list(XR) + [k(6)], writes=[k(7)])
            S.add("act", lambda e: e.activation(out=s, in_=t1, func=AF.Sin), reads=list(XR) + [k(6)], writes=[k(5)])
            S.add("act", lambda e: e.activation(out=c, in_=t2, func=AF.Sin), reads=list(XR) + [k(7)], writes=[k(4)])
            for _ in range(4):
                S.add("dve", lambda e: e.tensor_tensor(out=t1, in0=c, in1=c, op=ALU.mult), reads=list(XR) + [k(4)], writes=[k(6)])
                S.add("dve", lambda e: e.tensor_tensor(out=t2, in0=s, in1=s, op=ALU.mult), reads=list(XR) + [k(5)], writes=[k(7)])
                S.add("dve", lambda e: e.scalar_tensor_tensor(out=s, in0=c, scalar=2.0, in1=s, op0=ALU.mult, op1=ALU.mult),
                      reads=list(XR) + [k(4), k(5)], writes=[k(5)])
                S.add("dve", lambda e: e.tensor_tensor(out=c, in0=t1, in1=t2, op=ALU.subtract), reads=list(XR) + [k(6), k(7)], writes=[k(4)])
            return dict(rho=(a, k(3)), c=(c, k(4)), s=(s, k(5)), lr=(lr, k(0)), li=(li, k(1)), t1=(t1, k(6)), t2=(t2, k(7)))

        scr_s = [scrB[:, 144 + 16 * i:160 + 16 * i] for i in range(8)]
        Ts = lam_tables("ls", lamre_s, lamim_s, logdt_s, 16, scr_s)
        S.add("pool", lambda e: e.tensor_copy(out=rho_s[:], in_=Ts["rho"][0]), reads=[Ts["rho"][1]], writes=["rho_s"])
        S.add("dve", lambda e: e.tensor_tensor(out=lb_re[:], in0=Ts["rho"][0], in1=Ts["c"][0], op=ALU.mult),
              reads=[Ts["rho"][1], Ts["c"][1]], writes=["lb_re"])
        S.add("dve", lambda e: e.tensor_tensor(out=lb_im[:], in0=Ts["rho"][0], in1=Ts["s"][0], op=ALU.mult),
              reads=[Ts["rho"][1], Ts["s"][1]], writes=["lb_im"])
        Ec = scrB[:, 272:288]
        Es = scrB[:, 288:304]
        Et1 = scrB[:, 304:320]
        Et2 = scrB[:, 320:336]
        S.add("dve", lambda e: e.tensor_copy(out=Ec, in_=Ts["c"][0]), reads=[Ts["c"][1]], writes=["Ec"])
        S.add("dve", lambda e: e.tensor_copy(out=Es, in_=Ts["s"][0]), reads=[Ts["s"][1]], writes=["Es"])
        S.add("dve", lambda e: e.tensor_copy(out=cosT[:, :, 0], in_=Ec), reads=["Ec"], writes=["cosT"])
        S.add("dve", lambda e: e.tensor_copy(out=sinT[:, :, 0], in_=Es), reads=["Es"], writes=["sinT"])
        rtA = vg[:, :].rearrange("p (q j) -> p q j", q=16)
        rtB = x_sb[0][:, :].rearrange("p (q j) -> p q j", q=16)
        for kk in range(7):
            step = 1 << kk
            ecb = Ec.unsqueeze(2).broadcast_to([128, 16, step])
            esb = Es.unsqueeze(2).broadcast_to([128, 16, step])
            srcc = cosT[:, :, 0:step]
            srcs = sinT[:, :, 0:step]
            dstc = cosT[:, :, step:2 * step]
            dsts = sinT[:, :, step:2 * step]
            A = rtA[:, :, 0:step]
            Bq = rtB[:, :, 0:step]
            S.add("dve", lambda e, A=A, srcc=srcc, ecb=ecb: e.tensor_tensor(out=A, in0=srcc, in1=ecb, op=ALU.mult),
                  reads=["cosT", "Ec"], writes=[("vg", 0), ("vg", 1)])
            S.add("dve", lambda e, Bq=Bq, srcs=srcs, esb=esb: e.tensor_tensor(out=Bq, in0=srcs, in1=esb, op=ALU.mult),
                  reads=["sinT", "Es"], writes=[("x", 0)])
            S.add("dve", lambda e, A=A, Bq=Bq, dstc=dstc: e.tensor_tensor(out=dstc, in0=A, in1=Bq, op=ALU.subtract),
                  reads=[("vg", 0), ("vg", 1), ("x", 0)], writes=["cosT"])
            S.add("dve", lambda e, A=A, srcc=srcc, esb=esb: e.tensor_tensor(out=A, in0=srcc, in1=esb, op=ALU.mult),
                  reads=["cosT", "Es"], writes=[("vg", 0), ("vg", 1)])
            S.add("dve", lambda e, Bq=Bq, srcs=srcs, ecb=ecb: e.tensor_tensor(out=Bq, in0=srcs, in1=ecb, op=ALU.mult),
                  reads=["sinT", "Ec"], writes=[("x", 0)])
            S.add("dve", lambda e, A=A, Bq=Bq, dsts=dsts: e.tensor_tensor(out=dsts, in0=A, in1=Bq, op=ALU.add),
                  reads=[("vg", 0), ("vg", 1), ("x", 0)], writes=["sinT"])
            if kk < 6:
                S.add("dve", lambda e: e.tensor_tensor(out=Et1, in0=Ec, in1=Ec, op=ALU.mult), reads=["Ec"], writes=["Et1"])
                S.add("dve", lambda e: e.tensor_tensor(out=Et2, in0=Es, in1=Es, op=ALU.mult), reads=["Es"], writes=["Et2"])
                S.add("dve", lambda e: e.scalar_tensor_tensor(out=Es, in0=Ec, scalar=2.0, in1=Es, op0=ALU.mult, op1=ALU.mult),
                      reads=["Ec", "Es"], writes=["Es"])
                S.add("dve", lambda e: e.tensor_tensor(out=Ec, in0=Et1, in1=Et2, op=ALU.subtract), reads=["Et1", "Et2"], writes=["Ec"])

        S.add("act", lambda e: e.copy(out=cosB[:], in_=cosT[:]), reads=["cosT"], writes=["cosB"])
        S.add("act", lambda e: e.copy(out=sinB[:], in_=sinT[:]), reads=["sinT"], writes=["sinB"])
        scr_b = [x_sb[1][:, 256 * i:256 * i + 256] for i in range(4)] + [x_sb[2][:, 256 * i:256 * i + 256] for i in range(4)]
        XR = [("x", 1), ("x", 2), ("x", 3)]
        Tb = lam_tables("lb", lamre_b, lamim_b, logdt_b, 256, scr_b, XR)
        sc3 = [x_sb[3][:, 256 * i:256 * i + 256] for i in range(4)]
        bre_t, bim_t, nre, nim = sc3
        S.add("sp", lambda e: e.dma_start(out=bre_t, in_=Bre_b), reads=XR, writes=["bre_t"], dkey=("setup", "bre_t"))
        S.add("sp", lambda e: e.dma_start(out=bim_t, in_=Bim_b), reads=XR, writes=["bim_t"], dkey=("setup", "bim_t"))
        rho_b, c_b, s_b = Tb["rho"], Tb["c"], Tb["s"]
        lr_b, li_b = Tb["lr"], Tb["li"]
        t1_b, t2_b = Tb["t1"], Tb["t2"]
        S.add("dve", lambda e: e.tensor_tensor(out=t1_b[0], in0=rho_b[0], in1=c_b[0], op=ALU.mult), reads=XR + [rho_b[1], c_b[1]], writes=[t1_b[1]])
        S.add("dve", lambda e: e.tensor_scalar(out=t1_b[0], in0=t1_b[0], scalar1=-1.0, scalar2=None, op0=ALU.add), reads=XR + [t1_b[1]], writes=[t1_b[1]])
        S.add("dve", lambda e: e.tensor_tensor(out=t2_b[0], in0=rho_b[0], in1=s_b[0], op=ALU.mult), reads=XR + [rho_b[1], s_b[1]], writes=[t2_b[1]])
        S.add("dve", lambda e: e.tensor_tensor(out=nre, in0=t1_b[0], in1=lr_b[0], op=ALU.mult), reads=XR + [t1_b[1], lr_b[1]], writes=["nre"])
        S.add("dve", lambda e: e.tensor_tensor(out=c_b[0], in0=t2_b[0], in1=li_b[0], op=ALU.mult), reads=XR + [t2_b[1], li_b[1], c_b[1]], writes=[c_b[1]])
        S.add("dve", lambda e: e.tensor_tensor(out=nre, in0=nre, in1=c_b[0], op=ALU.add), reads=XR + ["nre", c_b[1]], writes=["nre"])
        S.add("dve", lambda e: e.tensor_tensor(out=nim, in0=t2_b[0], in1=lr_b[0], op=ALU.mult), reads=XR + [t2_b[1], lr_b[1]], writes=["nim"])
        S.add("dve", lambda e: e.tensor_tensor(out=c_b[0], in0=t1_b[0], in1=li_b[0], op=ALU.mult), reads=XR + [t1_b[1], li_b[1], c_b[1]], writes=[c_b[1]])
        S.add("dve", lambda e: e.tensor_tensor(out=nim, in0=nim, in1=c_b[0], op=ALU.subtract), reads=XR + ["nim", c_b[1]], writes=["nim"])
        S.add("dve", lambda e: e.tensor_tensor(out=c_b[0], in0=lr_b[0], in1=lr_b[0], op=ALU.mult), reads=XR + [lr_b[1], c_b[1]], writes=[c_b[1]])
        S.add("dve", lambda e: e.tensor_tensor(out=s_b[0], in0=li_b[0], in1=li_b[0], op=ALU.mult), reads=XR + [li_b[1], s_b[1]], writes=[s_b[1]])
        S.add("dve", lambda e: e.tensor_tensor(out=c_b[0], in0=c_b[0], in1=s_b[0], op=ALU.add), reads=XR + [c_b[1], s_b[1]], writes=[c_b[1]])
        S.add("dve", lambda e: e.reciprocal(out=c_b[0], in_=c_b[0]), reads=XR + [c_b[1]], writes=[c_b[1]])
        S.add("dve", lambda e: e.tensor_tensor(out=nre, in0=nre, in1=c_b[0], op=ALU.mult), reads=XR + ["nre", c_b[1]], writes=["nre"])
        S.add("dve", lambda e: e.tensor_tensor(out=nim, in0=nim, in1=c_b[0], op=ALU.mult), reads=XR + ["nim", c_b[1]], writes=["nim"])
        S.add("dve", lambda e: e.tensor_tensor(out=t1_b[0], in0=nre, in1=bre_t, op=ALU.mult), reads=XR + ["nre", "bre_t", t1_b[1]], writes=[t1_b[1]])
        S.add("dve", lambda e: e.tensor_tensor(out=s_b[0], in0=nim, in1=bim_t, op=ALU.mult), reads=XR + ["nim", "bim_t", s_b[1]], writes=[s_b[1]])
        S.add("dve", lambda e: e.tensor_tensor(out=t1_b[0], in0=t1_b[0], in1=s_b[0], op=ALU.subtract), reads=XR + [t1_b[1], s_b[1]], writes=[t1_b[1]])
        S.add("dve", lambda e: e.tensor_tensor(out=t2_b[0], in0=nre, in1=bim_t, op=ALU.mult), reads=XR + ["nre", "bim_t", t2_b[1]], writes=[t2_b[1]])
        S.add("dve", lambda e: e.tensor_tensor(out=s_b[0], in0=nim, in1=bre_t, op=ALU.mult), reads=XR + ["nim", "bre_t", s_b[1]], writes=[s_b[1]])
        S.add("dve", lambda e: e.tensor_tensor(out=t2_b[0], in0=t2_b[0], in1=s_b[0], op=ALU.add), reads=XR + [t2_b[1], s_b[1]], writes=[t2_b[1]])
        mB = scrB[:, 336:344]
        ld(mB, maskB, "mB")
        for kt in range(4):
            for part in range(2):
                src = (t1_b if part == 0 else t2_b)
                def f(e, kt=kt, part=part, src=src):
                    return e.tensor_tensor(
                        out=lhsT_BU[:, kt, :, part, :].rearrange("p q (g x) -> p q g x", g=2),
                        in0=src[0][:, kt * 64:(kt + 1) * 64].unsqueeze(1).unsqueeze(1).broadcast_to([128, 4, 2, 64]),
                        in1=mB.rearrange("p (q g) -> p q g", g=2).unsqueeze(3).broadcast_to([128, 4, 2, 64]),
                        op=ALU.mult)
                S.add("dve", f, reads=XR + [src[1], "mB"], writes=[("lhsT_BU", kt, part)])
        cl_re = scrC[:, 0:256]
        cl_im = scrC[:, 256:512]
        mC = scrC[:, 512:640]
        S.add("sp", lambda e: e.dma_start(out=cl_re, in_=Cre_l), writes=[("x", 0)], dkey=("setup", "cl_re"))
        S.add("sp", lambda e: e.dma_start(out=cl_im, in_=Cim_l), reads=[("x", 0)], writes=["cl_im"], dkey=("setup", "cl_im"))
        S.add("sp", lambda e: e.dma_start(out=mC, in_=maskC), reads=[("x", 0)], writes=["mC"], dkey=("setup", "mC"))
        S.add("dve", lambda e: e.tensor_scalar(out=cl_im, in0=cl_im, scalar1=-1.0, scalar2=None, op0=ALU.mult), reads=["cl_im", ("x", 0)], writes=["cl_im"])
        for part in range(2):
            src, skey = (cl_re, ("x", 0)) if part == 0 else (cl_im, "cl_im")
            def f(e, part=part, src=src):
                return e.tensor_tensor(
                    out=lhsT_C[:, :, part, :].rearrange("p q (j h) -> p q j h", h=16),
                    in0=src.rearrange("p (q h) -> p q h", h=16).unsqueeze(2).broadcast_to([128, 16, 8, 16]),
                    in1=mC.rearrange("p (q j) -> p q j", j=8).unsqueeze(3).broadcast_to([128, 16, 8, 16]),
                    op=ALU.mult)
            S.add("dve", f, reads=[skey, "mC", ("x", 0)], writes=[("lhsT_C", part)])
        LBU_KEYS = [("lhsT_BU", kt, part) for kt in range(4) for part in range(2)]
        LC_KEYS = [("lhsT_C", 0), ("lhsT_C", 1)]
        DD_KEYS = [("diagD", m) for m in range(4)]

        SCRB_KEYS = ["trimask", "wdiag", "dvec", "mB", "Ec", "Es", "Et1", "Et2"] + ["ls_%d" % i for i in range(8)]
        S.add("dve", lambda e: e.memset(tmp_f[1][0:1, 0:1], 0.0), writes=SCRB_KEYS + [("tmpf", 1)])

        ITEM_SEQ = []
        for pi_ in range(NPASS):
            if pi_ == 0:
                ITEM_SEQ.append((0, "s"))
            ITEM_SEQ += [(pi_, n_) for n_ in ["u0", "u1", "v0", "v1", "ga0", "ga1", "wa0", "wa1", "gb0", "gb1"]]
            if pi_ + 1 < NPASS:
                ITEM_SEQ.append((pi_ + 1, "s"))
            ITEM_SEQ += [(pi_, n_) for n_ in ["wb0", "wb1", "wo0", "wo1"]]
            ITEM_SEQ += [(pi_, "gu%d" % c_) for c_ in range(11)]
            ITEM_SEQ += [(pi_, "d%d" % j_) for j_ in range(6)]
        GIDX = {k_: i_ for i_, k_ in enumerate(ITEM_SEQ)}

        def slot_keys(s_):
            return [("ring", s_, 0), ("ring", s_, 1)]

        def item_dmas(name, s_):
            sl = ring[:, s_, :]
            v8_ = sl.rearrange("p (kt n) -> p kt n", kt=8)
            WIN = {"s": 2048, "u0": 0, "u1": 512, "v0": 1024, "v1": 1536, "ga0": 2560, "ga1": 3072, "gb0": 3584, "gb1": 4096}
            if name in WIN:
                c0 = WIN[name]
                return [(v8_, w_in[:, c0:c0 + 512].rearrange("(kt p) n -> p kt n", p=128), None)]
            if name.startswith("wa"):
                c0 = int(name[2:]) * 512
                return [(v8_, w_a[:, c0:c0 + 512].rearrange("(kt p) n -> p kt n", p=128), None)]
            if name.startswith("wb"):
                c0 = int(name[2:]) * 1024
                return [(sl.rearrange("p (kt n) -> p kt n", kt=4), w_b[:, c0:c0 + 1024].rearrange("(kt p) n -> p kt n", p=128), None)]
            if name.startswith("wo"):
                c0 = int(name[2:]) * 512
                return [(v8_, w_out[:, c0:c0 + 512].rearrange("(kt p) n -> p kt n", p=128), None)]
            if name.startswith("gu"):
                c0 = int(name[2:]) * 256
                v_ = sl.rearrange("p (two kt n) -> p two kt n", two=2, kt=8)
                return [(v_[:, 0], w_g[:, c0:c0 + 256].rearrange("(kt p) n -> p kt n", p=128), 0),
                        (v_[:, 1], w_u[:, c0:c0 + 256].rearrange("(kt p) n -> p kt n", p=128), 1)]
            j_ = int(name[1:])
            hh_, part_ = j_ // 3, j_ % 3
            nk = 8 if part_ < 2 else 6
            r0 = part_ * 1024
            return [(v8_[:, 0:nk, :], w_d[r0:r0 + nk * 128, hh_ * 512:(hh_ + 1) * 512].rearrange("(kt p) n -> p kt n", p=128), None)]

        issued = [0]
        consumed = [0]

        NAME_IDX = {}
        for (_p, _n) in ITEM_SEQ:
            if _n not in NAME_IDX:
                NAME_IDX[_n] = len(NAME_IDX)
        wscr = nc.dram_tensor("wscr", [len(NAME_IDX), 128, 4096], BF16).ap()
        first_seen = set()

        def issue_next():
            g = issued[0]
            if g >= len(ITEM_SEQ):
                return
            issued[0] += 1
            s_ = g % NSLOT
            name = ITEM_SEQ[g][1]
            idx = NAME_IDX[name]
            if name not in first_seen:
                first_seen.add(name)
                for dst, src, half in item_dmas(name, s_):
                    if half is None:
                        wk = slot_keys(s_)
                        dk = ("ring", s_, 0)
                    else:
                        wk = [("ring", s_, half)]
                        dk = ("ring", s_, half)
                    S.add("pool", lambda e, dst=dst, src=src: e.dma_start(out=dst, in_=src), writes=wk, dkey=dk)
                S.add("sp", lambda e, s_=s_, idx=idx: e.dma_start(out=wscr[idx], in_=ring[:, s_, :]),
                      reads=slot_keys(s_), writes=[("scr", idx)], dkey=("wbk", s_))
            else:
                S.add("sp", lambda e, s_=s_, idx=idx: e.dma_start(out=ring[:, s_, :], in_=wscr[idx]),
                      reads=[("scr", idx)], writes=slot_keys(s_), dkey=("ringhw", s_))

        def done_item(pi_, name):
            assert GIDX[(pi_, name)] == consumed[0], (pi_, name, consumed[0])
            consumed[0] += 1
            issue_next()

        def slot_of(pi_, name):
            g = GIDX[(pi_, name)]
            assert consumed[0] <= g < issued[0], (pi_, name, g, consumed[0], issued[0])
            return g % NSLOT

        for _ in range(NSLOT):
            issue_next()

        stat_ctr = [0]

        def next_stat():
            i = stat_ctr[0] % 4
            stat_ctr[0] += 1
            return stat[i], ("stat", i)

        tmp_ctr = [0]

        def next_tmp():
            i = tmp_ctr[0] % 2
            tmp_ctr[0] += 1
            return i

        htm_ctr = [0]

        def next_htm():
            i = htm_ctr[0] % 2
            htm_ctr[0] += 1
            return i

        def rms_rstd(xap, xkeys, TS, hi):
            stt, sk = next_stat()
            S.add("act", lambda e: e.activation(out=h_tm2[hi][:TS, :], in_=xap, func=AF.Square, accum_out=stt[:TS, 0:1]),
                  reads=list(xkeys), writes=[("h_tm", hi), sk])
            S.add("act", lambda e: e.activation(out=stt[:TS, 2:3], in_=stt[:TS, 0:1], func=AF.Ln, scale=1.0 / D, bias=EPS), reads=[sk], writes=[sk])
            S.add("act", lambda e: e.activation(out=stt[:TS, 3:4], in_=stt[:TS, 2:3], func=AF.Exp, scale=-0.5), reads=[sk], writes=[sk])
            return stt[:TS, 3:4], sk

        def norm_chain(xap, xkeys, TS):
            hi = next_htm()
            rstd, sk = rms_rstd(xap, xkeys, TS, hi)
            S.add("act", lambda e: e.activation(out=h_tm2[hi][:TS, :], in_=xap, func=AF.Identity, scale=rstd),
                  reads=list(xkeys) + [sk], writes=[("h_tm", hi)])
            return hi

        def norm_tr(hi, tt, TS, gT, gkey, dstT, dkeyname):
            def tr(e):
                for kt in range(8):
                    ins = e.transpose(out=pT[:, kt, :TS], in_=h_tm2[hi][:TS, kt * 128:(kt + 1) * 128], identity=ident_b[:TS, :TS])
                return ins
            S.add("pe", tr, reads=[("h_tm", hi), "ident_b"], writes=["pT"])
            S.add("dve", lambda e: e.tensor_tensor(out=dstT[:, :, tt * 128:tt * 128 + TS], in0=pT[:, :, :TS],
                                                   in1=gT[:, :].unsqueeze(2).broadcast_to([128, 8, TS]), op=ALU.mult),
                  reads=["pT", gkey], writes=[(dkeyname, tt)])

        class Ctx:
            def __init__(self, pi, sample, tok0):
                self.pi = pi
                self.sample = sample
                self.NT = 16 if sample else 512
                self.TS = 16 if sample else 128
                self.NTT = 1 if sample else 4
                self.tok0 = tok0

        def v8(c, name):
            return ring[:, slot_of(c.pi, name), :].rearrange("p (kt n) -> p kt n", kt=8)

        def rk(c, name):
            return slot_keys(slot_of(c.pi, name))

        def fm_group(c, name, mcol, nk, rhs_fn, rhs_keys, view=None):
            bank, bkey = next_bank()
            w = view if view is not None else v8(c, name)
            NT = c.NT

            def f(e):
                for k in range(nk):
                    ins = e.matmul(bank[:, :NT], lhsT=w[:, k, mcol * 128:(mcol + 1) * 128], rhs=rhs_fn(k), start=(k == 0), stop=(k == nk - 1))
                return ins
            S.add("pe", f, reads=rk(c, name) + rhs_keys, writes=[bkey])
            return bank, bkey

        def pre_units(c):
            chains, trs = [], []
            for tt in range(c.NTT):
                st_ = {}

                def chain(tt=tt, st_=st_):
                    TS = c.TS
                    src = xs if c.sample else xp[c.tok0 + tt * 128: c.tok0 + (tt + 1) * 128, :]
                    XK = [("vg", 0), ("vg", 1)]
                    S.add("sp", lambda e: e.dma_start(out=vg[:TS, :], in_=src), writes=XK, dkey="xpre")
                    st_["hi"] = norm_chain(vg[:TS, :], XK, TS)

                def trp(tt=tt, st_=st_):
                    norm_tr(st_["hi"], tt, c.TS, gmT, "gmT", hTa, "hTa")
                chains.append(chain)
                trs.append(trp)
            return chains, trs

        def sblock_unit(c):
            HK = [("hTa", tt) for tt in range(c.NTT)]
            for m in range(4):
                bank, bkey = fm_group(c, "s", m, 8, lambda k: hTa[:, k, :c.NT], HK)
                S.add("act", lambda e, m=m, bank=bank: e.copy(out=sT[:, m, :c.NT], in_=bank[:, :c.NT]), reads=[bkey], writes=[("sT", m)])
            done_item(c.pi, "s")

        def ssm_units(c):
            ssm = []
            if c.sample:
                ssm.append(lambda: ssm_sample_unit(c))
                return ssm
            unitsA, unitsB, unitsB2, unitsC = [], [], [], []
            for tt in range(c.NTT):
                for m in range(4):
                    idx = tt * 4 + m
                    st_ = idx % 2
                    T6 = tS[6 * st_:6 * st_ + 6]
                    K6 = ["tS%d" % (6 * st_ + i) for i in range(6)]
                    cs = cosT[:, 4 * m:4 * m + 4, :]
                    sn = sinT[:, 4 * m:4 * m + 4, :]

                    def unit_a(tt=tt, m=m, T6=T6, K6=K6, cs=cs, sn=sn):
                        def f(e):
                            for ql in range(4):
                                for part in range(2):
                                    bank = bRe if part == 0 else bIm
                                    ins = e.matmul(bank[:, ql * 128:(ql + 1) * 128], lhsT=lhsT_BU[:, m, ql, part, :], rhs=sT[:, m, tt * 128:(tt + 1) * 128], start=True, stop=True)
                            return ins
                        S.add("pe", f, reads=LBU_KEYS + [("sT", m)], writes=["bRe", "bIm"])
                        re3 = bRe[:].rearrange("p (q t) -> p q t", q=4)
                        im3 = bIm[:].rearrange("p (q t) -> p q t", q=4)
                        tA, tB, tC, tD, tE, tF = T6
                        S.add("dve", lambda e: e.tensor_tensor(out=tD, in0=re3, in1=sn, op=ALU.mult), reads=["bRe", "sinT"], writes=[K6[3]])
                        S.add("dve", lambda e: e.tensor_tensor(out=tA, in0=re3, in1=cs, op=ALU.mult), reads=["bRe", "cosT"], writes=[K6[0]])
                        S.add("dve", lambda e: e.tensor_tensor(out=tB, in0=im3, in1=sn, op=ALU.mult), reads=["bIm", "sinT"], writes=[K6[1]])
                        S.add("dve", lambda e: e.tensor_tensor(out=tC, in0=im3, in1=cs, op=ALU.mult), reads=["bIm", "cosT"], writes=[K6[2]])
                        S.add("dve", lambda e: e.tensor_tensor(out=tA, in0=tA, in1=tB, op=ALU.add), reads=[K6[0], K6[1]], writes=[K6[0]])
                        S.add("dve", lambda e: e.tensor_tensor(out=tC, in0=tC, in1=tD, op=ALU.subtract), reads=[K6[2], K6[3]], writes=[K6[2]])

                    def unit_b(tt=tt, m=m, T6=T6, K6=K6, cs=cs, sn=sn, idx=idx, st_=st_):
                        tA, tB, tC, tD, tE, tF = T6

                        def fs(e):
                            for ql in range(4):
                                q = 4 * m + ql
                                e.tensor_tensor_scan(out=tB[:, ql, :], data0=rho_s[:, q:q + 1].broadcast_to([128, 128]), data1=tA[:, ql, :],
                                                     initial=hst[0][:, q:q + 1], op0=ALU.mult, op1=ALU.add)
                                ins = e.tensor_tensor_scan(out=tD[:, ql, :], data0=rho_s[:, q:q + 1].broadcast_to([128, 128]), data1=tC[:, ql, :],
                                                           initial=hst[1][:, q:q + 1], op0=ALU.mult, op1=ALU.add)
                            return ins
                        S.add("dve", fs, reads=[K6[0], K6[2], "rho_s", ("hst", 0, m), ("hst", 1, m)], writes=[K6[1], K6[3]])

                    def unit_b2(tt=tt, m=m, T6=T6, K6=K6, cs=cs, sn=sn, idx=idx, st_=st_):
                        tA, tB, tC, tD, tE, tF = T6
                        def bview(i):
                            v = tSall[:, 6 * st_ + i, :].bitcast(BF16)
                            return (v[:, 0:512].rearrange("p (a b) -> p a b", a=4), v[:, 512:1024].rearrange("p (a b) -> p a b", a=4))
                        rb_re, rb_im = bview(4)
                        u1, u2 = bview(5)
                        u3, u4 = bview(0)
                        cb = cosB[:, 4 * m:4 * m + 4, :]
                        sbq = sinB[:, 4 * m:4 * m + 4, :]
                        S.add("act", lambda e: e.copy(out=rb_re, in_=tB), reads=[K6[1]], writes=[K6[4]])
                        S.add("act", lambda e: e.copy(out=rb_im, in_=tD), reads=[K6[3], K6[4]], writes=[K6[4]])
                        p = sst[:, st_, :]
                        pk = ("sst", st_)
                        S.add("pool", lambda e: e.tensor_tensor(out=p[:, 0:4], in0=tB[:, :, 127], in1=cosT[:, 4 * m:4 * m + 4, 127], op=ALU.mult), reads=[K6[1], "cosT"], writes=[pk])
                        S.add("pool", lambda e: e.tensor_tensor(out=p[:, 4:8], in0=tD[:, :, 127], in1=sinT[:, 4 * m:4 * m + 4, 127], op=ALU.mult), reads=[K6[3], "sinT", pk], writes=[pk])
                        S.add("pool", lambda e: e.tensor_tensor(out=p[:, 8:12], in0=tB[:, :, 127], in1=sinT[:, 4 * m:4 * m + 4, 127], op=ALU.mult), reads=[K6[1], "sinT", pk], writes=[pk])
                        S.add("pool", lambda e: e.tensor_tensor(out=p[:, 12:16], in0=tD[:, :, 127], in1=cosT[:, 4 * m:4 * m + 4, 127], op=ALU.mult), reads=[K6[3], "cosT", pk], writes=[pk])
                        S.add("pool", lambda e: e.tensor_tensor(out=hst[0][:, 4 * m:4 * m + 4], in0=p[:, 0:4], in1=p[:, 4:8], op=ALU.subtract), reads=[pk], writes=[("hst", 0, m)])
                        S.add("pool", lambda e: e.tensor_tensor(out=hst[1][:, 4 * m:4 * m + 4], in0=p[:, 8:12], in1=p[:, 12:16], op=ALU.add), reads=[pk], writes=[("hst", 1, m)])
                        S.add("dve", lambda e: e.tensor_tensor(out=u1, in0=rb_re, in1=cb, op=ALU.mult), reads=[K6[4], "cosB"], writes=[K6[5]])
                        S.add("dve", lambda e: e.tensor_tensor(out=u2, in0=rb_im, in1=sbq, op=ALU.mult), reads=[K6[4], "sinB", K6[5]], writes=[K6[5]])
                        S.add("dve", lambda e: e.tensor_tensor(out=u3, in0=rb_re, in1=sbq, op=ALU.mult), reads=[K6[4], "sinB"], writes=[K6[0]])
                        S.add("dve", lambda e: e.tensor_tensor(out=u4, in0=rb_im, in1=cb, op=ALU.mult), reads=[K6[4], "cosB", K6[0]], writes=[K6[0]])
                        hb = h_bf[idx % 2]
                        hk = ("hbf", idx % 2)
                        S.add("dve", lambda e: e.tensor_tensor(out=hb[:, 0, :, :], in0=u1, in1=u2, op=ALU.subtract), reads=[K6[5]], writes=[(hk, 0)])
                        S.add("dve", lambda e: e.tensor_tensor(out=hb[:, 1, :, :], in0=u3, in1=u4, op=ALU.add), reads=[K6[0]], writes=[(hk, 1)])

                    def unit_cp(tt=tt, m=m, idx=idx):
                        hb = h_bf[idx % 2]
                        hk = ("hbf", idx % 2)

                        def fc(e):
                            o = bY[:, m * 128:(m + 1) * 128]
                            first = True
                            for ql in range(4):
                                for part in range(2):
                                    e.matmul(o, lhsT=lhsT_C[:, 4 * m + ql, part, :], rhs=hb[:, part, ql, :], start=first, stop=False)
                                    first = False
                            return e.matmul(o, lhsT=diagD[:, m, :], rhs=sT[:, m, tt * 128:(tt + 1) * 128], start=False, stop=True)
                        S.add("pe", fc, reads=LC_KEYS + DD_KEYS + [(hk, 0), (hk, 1), ("sT", m)], writes=["bY"])
                        if m == 3:
                            S.add("act", lambda e: e.copy(out=yT[:, :, tt * 128:(tt + 1) * 128], in_=bY[:].rearrange("p (m t) -> p m t", m=4)),
                                  reads=["bY"], writes=[("yT", tt)])
                    unitsA.append(unit_a)
                    unitsB.append(unit_b)
                    unitsB2.append(unit_b2)
                    unitsC.append(unit_cp)
            nq = len(unitsA)
            for i in range(nq + 2):
                def step(i=i):
                    if 1 <= i <= nq:
                        unitsB[i - 1]()
                    if i < nq:
                        unitsA[i]()
                    if 1 <= i <= nq:
                        unitsB2[i - 1]()
                    if i >= 2:
                        unitsC[i - 2]()
                ssm.append(step)
            return ssm

        def ssm_sample_unit(c):
            VGK = [("vg", 0), ("vg", 1)]
            for part, src in enumerate([h0re, h0im]):
                bank = bRe if part == 0 else bIm
                bkey = "bRe" if part == 0 else "bIm"
                for half in range(2):
                    S.add("sp", lambda e, src=src, half=half: e.dma_start(out=vg[:16, :], in_=src[:, half * 1024:(half + 1) * 1024]),
                          writes=VGK, dkey="h0ld")

                    def f(e, half=half, bank=bank):
                        for q8 in range(8):
                            q = half * 8 + q8
                            ins = e.transpose(out=bank[:, q * 16:(q + 1) * 16], in_=vg[:16, q8 * 128:(q8 + 1) * 128], identity=ident_f[:16, :16])
                        return ins
                    S.add("pe", f, reads=VGK + ["ident_f"], writes=[bkey])
                S.add("act", lambda e, part=part, bank=bank: e.copy(out=h0T[part], in_=bank[:, 0:256].rearrange("p (q t) -> p q t", q=16)),
                      reads=[bkey], writes=["tS%d" % (6 + part)])

            def f(e):
                for q in range(16):
                    for part in range(2):
                        bank = bRe if part == 0 else bIm
                        ins = e.matmul(bank[:, q * 16:(q + 1) * 16], lhsT=lhsT_BU[:, q // 4, q % 4, part, :], rhs=sT[:, q // 4, :16], start=True, stop=True)
                return ins
            S.add("pe", f, reads=LBU_KEYS + [("sT", m) for m in range(4)], writes=["bRe", "bIm"])
            lre = lb_re[:, :].unsqueeze(2).broadcast_to([128, 16, 16])
            lim = lb_im[:, :].unsqueeze(2).broadcast_to([128, 16, 16])
            a1 = tSall[:, 0, 0:256].rearrange("p (q t) -> p q t", q=16)
            a2 = tSall[:, 1, 0:256].rearrange("p (q t) -> p q t", q=16)
            bre3 = bRe[:, 0:256].rearrange("p (q t) -> p q t", q=16)
            bim3 = bIm[:, 0:256].rearrange("p (q t) -> p q t", q=16)
            S.add("dve", lambda e: e.tensor_tensor(out=a1, in0=h0T[0], in1=lre, op=ALU.mult), reads=["tS6", "lb_re"], writes=["tS0"])
            S.add("dve", lambda e: e.tensor_tensor(out=a2, in0=h0T[1], in1=lim, op=ALU.mult), reads=["tS7", "lb_im"], writes=["tS1"])
            S.add("dve", lambda e: e.tensor_tensor(out=a1, in0=a1, in1=a2, op=ALU.subtract), reads=["tS0", "tS1"], writes=["tS0"])
            S.add("dve", lambda e: e.tensor_tensor(out=hs[0], in0=bre3, in1=a1, op=ALU.add), reads=["tS0", "bRe"], writes=["tS8"])
            S.add("dve", lambda e: e.tensor_tensor(out=a1, in0=h0T[1], in1=lre, op=ALU.mult), reads=["tS7", "lb_re"], writes=["tS0"])
            S.add("dve", lambda e: e.tensor_tensor(out=a2, in0=h0T[0], in1=lim, op=ALU.mult), reads=["tS6", "lb_im"], writes=["tS1"])
            S.add("dve", lambda e: e.tensor_tensor(out=a1, in0=a1, in1=a2, op=ALU.add), reads=["tS0", "tS1"], writes=["tS0"])
            S.add("dve", lambda e: e.tensor_tensor(out=hs[1], in0=bim3, in1=a1, op=ALU.add), reads=["tS0", "bIm"], writes=["tS9"])
            for part in range(2):
                S.add("act", lambda e, part=part: e.copy(out=hs_bf[:, :, part, :], in_=hs[part]), reads=["tS%d" % (8 + part)], writes=[("tmpb", 1)])

            def fc(e):
                for m in range(4):
                    o = bY[:, m * 128:m * 128 + 16]
                    first = True
                    for ql in range(4):
                        for part in range(2):
                            e.matmul(o, lhsT=lhsT_C[:, 4 * m + ql, part, :], rhs=hs_bf[:, 4 * m + ql, part, :], start=first, stop=False)
                            first = False
                    ins = e.matmul(o, lhsT=diagD[:, m, :], rhs=sT[:, m, :16], start=False, stop=True)
                return ins
            S.add("pe", fc, reads=LC_KEYS + DD_KEYS + [("tmpb", 1)] + [("sT", m) for m in range(4)], writes=["bY"])
            S.add("act", lambda e: e.copy(out=yT[:, :, 0:16], in_=bY[:].rearrange("p (m t) -> p m t", m=4)[:, :, 0:16]),
                  reads=["bY"], writes=[("yT", 0)])
            for part in range(2):
                dsto = sre_s if part == 0 else sim_s
                for grp in range(4):
                    bank, bkey = next_bank()

                    def ft(e, part=part, grp=grp, bank=bank):
                        for ql in range(4):
                            ins = e.transpose(out=bank[:16, ql * 128:(ql + 1) * 128], in_=hs[part][:, 4 * grp + ql, :], identity=ident_f[:, :])
                        return ins
                    S.add("pe", ft, reads=["tS%d" % (8 + part), "ident_f"], writes=[bkey])
                    S.add("act", lambda e, bank=bank, grp=grp: e.copy(out=vg[:16, (grp % 2) * 512:(grp % 2) * 512 + 512], in_=bank[:16, :]),
                          reads=[bkey], writes=[("vg", grp % 2)])
                    if grp % 2 == 1:
                        S.add("sp", lambda e, dsto=dsto, grp=grp: e.dma_start(out=dsto[:, (grp // 2) * 1024:(grp // 2 + 1) * 1024], in_=vg[:16, :]),
                              reads=VGK, dkey=("sout", part, grp // 2))

        def run_pass(c, nxt, own=None):
            NT, TS, NTT, sample = c.NT, c.TS, c.NTT, c.sample
            own = own or []
            own_i = [0]
            own_prog = [0]
            NOWN = 8 + 2 * NTT + 8 + 8 + 8

            def run_own(flush=False):
                own_prog[0] += 1
                tgt = len(own) if flush else (len(own) * own_prog[0] + NOWN - 1) // NOWN
                while own_i[0] < min(tgt, len(own)):
                    own[own_i[0]]()
                    own_i[0] += 1
            HK = [("hTa", tt) for tt in range(NTT)]
            hrhs = lambda k: hTa[:, k, :NT]
            for tt in range(NTT):
                src = xs if sample else xp[c.tok0 + tt * 128: c.tok0 + (tt + 1) * 128, :]
                S.add("sp", lambda e, tt=tt, src=src: e.dma_start(out=x_sb[tt][:TS, :], in_=src), writes=[("x", tt)], dkey=("x", tt))
            for m in range(8):
                name = "u%d" % (m // 4)
                bank, bkey = fm_group(c, name, m % 4, 8, hrhs, HK)
                S.add("act", lambda e, m=m, bank=bank: e.activation(out=big[:, m, :NT], in_=bank[:, :NT], func=AF.Gelu_apprx_tanh),
                      reads=[bkey], writes=[("big", m)])
                if m % 4 == 3:
                    done_item(c.pi, name)
                run_own()

            def v_unit(tt):
                vl = v_ln[tt % 2]
                vk = ("vln", tt % 2)
                for hh in range(2):
                    bank, bkey = next_bank()
                    w = v8(c, "v%d" % hh)

                    def f(e, w=w, bank=bank):
                        for k in range(8):
                            ins = e.matmul(bank[:TS, :], lhsT=hTa[:, k, tt * 128:tt * 128 + TS], rhs=w[:, k, :], start=(k == 0), stop=(k == 7))
                        return ins
                    S.add("pe", f, reads=rk(c, "v%d" % hh) + [("hTa", tt)], writes=[bkey])
                    S.add("act", lambda e, hh=hh, bank=bank: e.activation(out=vg[:TS, hh * 512:(hh + 1) * 512], in_=bank[:TS, :], func=AF.Gelu_apprx_tanh),
                          reads=[bkey], writes=[("vg", hh)])
                stt, sk = next_stat()
                S.add("dve", lambda e: e.bn_stats(out=stt[:TS, 0:6], in_=vg[:TS, 0:512]), reads=[("vg", 0)], writes=[sk])
                S.add("dve", lambda e: e.bn_stats(out=stt[:TS, 6:12], in_=vg[:TS, 512:1024]), reads=[("vg", 1)], writes=[sk])
                S.add("dve", lambda e: e.bn_aggr(out=stt[:TS, 12:14], in_=stt[:TS, 0:12]), reads=[sk], writes=[sk])
                S.add("act", lambda e: e.activation(out=stt[:TS, 14:15], in_=stt[:TS, 13:14], func=AF.Ln, bias=EPS), reads=[sk], writes=[sk])
                S.add("act", lambda e: e.activation(out=stt[:TS, 14:15], in_=stt[:TS, 14:15], func=AF.Exp, scale=-0.5), reads=[sk], writes=[sk])
                S.add("dve", lambda e: e.tensor_scalar(out=stt[:TS, 15:16], in0=stt[:TS, 12:13], scalar1=stt[:TS, 14:15], scalar2=-1.0, op0=ALU.mult, op1=ALU.mult), reads=[sk], writes=[sk])
                VG = [("vg", 0), ("vg", 1)]
                S.add("act", lambda e: e.activation(out=vg[:TS, :], in_=vg[:TS, :], func=AF.Identity, scale=stt[:TS, 14:15], bias=stt[:TS, 15:16]),
                      reads=[sk] + VG, writes=VG)
                S.add("dve", lambda e: e.tensor_tensor(out=vg[:TS, :], in0=vg[:TS, :], in1=lng_t[:TS, :], op=ALU.mult), reads=VG + ["lng_t"], writes=VG)
                S.add("pool", lambda e: e.tensor_tensor(out=vg[:TS, :], in0=vg[:TS, :], in1=lnb_t[:TS, :], op=ALU.add), reads=VG + ["lnb_t"], writes=VG)
                S.add("act", lambda e: e.copy(out=vl[:TS, :], in_=vg[:TS, :]), reads=VG, writes=[vk])
                if sample:
                    S.add("sp", lambda e: e.dma_start(out=v_s, in_=vg[:TS, :]), reads=VG, dkey="v_s")

            def sp_unit(tt):
                vl = v_ln[tt % 2]
                vk = ("vln", tt % 2)
                for gh in range(2):
                    bank, bkey = next_bank()

                    def f(e, gh=gh, bank=bank):
                        for gl in range(4):
                            g = 4 * gh + gl
                            if sample:
                                rhs = WspT_s[:16, g, :16]
                                brhs = bsps_bf[0:1, g, :16]
                            else:
                                rhs = WspT[:, g, :]
                                brhs = bsp_bf[0:1, g, :]
                            o = bank[:, gl * 128:gl * 128 + TS]
                            e.matmul(o, lhsT=vl[:TS, g * 128:(g + 1) * 128], rhs=rhs, start=True, stop=False)
                            ins = e.matmul(o, lhsT=ones_bf[0:1, :], rhs=brhs, start=False, stop=True)
                        return ins
                    S.add("pe", f, reads=[vk, "WspT", "WspT_s", "bsp_bf", "bsps_bf", "ones_bf"], writes=[bkey])
                    GK = [("big", 4 * gh + gl) for gl in range(4)]
                    S.add("dve", lambda e, gh=gh, bank=bank: e.tensor_tensor(
                        out=big[:, 4 * gh:4 * gh + 4, tt * 128:tt * 128 + TS],
                        in0=bank[:].rearrange("p (g t) -> p g t", g=4)[:, :, :TS],
                        in1=big[:, 4 * gh:4 * gh + 4, tt * 128:tt * 128 + TS], op=ALU.mult),
                        reads=[bkey] + GK, writes=GK)

            def ga_unit(m):
                name = "ga%d" % (m // 4)
                bank, bkey = fm_group(c, name, m % 4, 8, hrhs, HK)
                S.add("act", lambda e: e.activation(out=big[:, 8 + m, :NT], in_=bank[:, :NT], func=AF.Sigmoid), reads=[bkey], writes=[("big", 8 + m)])
                run_own()

            ga_i = 0
            for tt in range(NTT):
                v_unit(tt)
                run_own()
                if tt >= 1:
                    for _ in range(2):
                        if ga_i < 4:
                            ga_unit(ga_i)
                            ga_i += 1
                    sp_unit(tt - 1)
                    run_own()
            done_item(c.pi, "v0")
            done_item(c.pi, "v1")
            while ga_i < 4:
                ga_unit(ga_i)
                ga_i += 1
            done_item(c.pi, "ga0")
            ga_unit(4)
            ga_unit(5)
            sp_unit(NTT - 1)
            ga_unit(6)
            ga_unit(7)
            done_item(c.pi, "ga1")
            GKall = [("big", k) for k in range(8)]
            for m in range(8):
                name = "wa%d" % (m // 4)
                bank, bkey = fm_group(c, name, m % 4, 8, lambda k: big[:, k, :NT], GKall)
                S.add("dve", lambda e, m=m, bank=bank: e.tensor_tensor(out=big[:, 8 + m, :NT], in0=bank[:, :NT], in1=big[:, 8 + m, :NT], op=ALU.mult),
                      reads=[bkey, ("big", 8 + m)], writes=[("big", 8 + m)])
                if m % 4 == 3:
                    done_item(c.pi, name)
                run_own()
            if nxt is not None:
                chains, trs = pre_units(nxt)
            else:
                chains, trs = [], []
            for m in range(8):
                name = "gb%d" % (m // 4)
                if m == 4 and chains:
                    chains[0]()
                bank, bkey = fm_group(c, name, m % 4, 8, hrhs, HK)
                S.add("act", lambda e, m=m, bank=bank: e.activation(out=big[:, m, :NT], in_=bank[:, :NT], func=AF.Sigmoid), reads=[bkey], writes=[("big", m)])
                if m % 4 == 3:
                    done_item(c.pi, name)
                run_own()
            run_own(flush=True)
            if len(chains) > 1:
                chains[1]()
            side = []
            side_i = [0]
            prog = [0]
            NMAIN = NTT + 11 + 2 * NTT

            def run_side():
                prog[0] += 1
                ntot = NMAIN + (40 if (nxt is not None and not nxt.sample) else 0)
                tgt = (len(side) * prog[0] + ntot - 1) // ntot
                while side_i[0] < min(tgt, len(side)):
                    side[side_i[0]]()
                    side_i[0] += 1

            YK = [("yT", tt) for tt in range(NTT)]

            def wb_unit(m):
                vb0 = ring[:, slot_of(c.pi, "wb0"), :].rearrange("p (kt n) -> p kt n", kt=4)
                vb1 = ring[:, slot_of(c.pi, "wb1"), :].rearrange("p (kt n) -> p kt n", kt=4)
                bV, kV = fm_group(c, "wb0", m, 4, lambda k: yT[:, k, :NT], YK, view=vb0)
                bG, kG = fm_group(c, "wb1", m, 4, lambda k: yT[:, k, :NT], YK, view=vb1)
                ti = next_tmp()
                S.add("act", lambda e: e.activation(out=tmp_f[ti][:, :NT], in_=bG[:, :NT], func=AF.Sigmoid), reads=[kG], writes=[("tmpf", ti)])
                S.add("dve", lambda e: e.tensor_tensor(out=tmp_b[ti][:, :NT], in0=bV[:, :NT], in1=tmp_f[ti][:, :NT], op=ALU.mult),
                      reads=[kV, ("tmpf", ti)], writes=[("tmpb", ti)])
                S.add("dve", lambda e: e.tensor_tensor(out=tmp_b[ti][:, :NT], in0=tmp_b[ti][:, :NT], in1=big[:, m, :NT], op=ALU.mult),
                      reads=[("tmpb", ti), ("big", m)], writes=[("tmpb", ti)])
                S.add("dve", lambda e: e.tensor_tensor(out=big[:, m, :NT], in0=tmp_b[ti][:, :NT], in1=big[:, 8 + m, :NT], op=ALU.add),
                      reads=[("tmpb", ti), ("big", 8 + m)], writes=[("big", m)])

            npre = len(trs)
            for m in range(8):
                if m % 2 == 0:
                    pidx = m // 2
                    if pidx < npre:
                        trs[pidx]()
                        if pidx + 2 < npre:
                            chains[pidx + 2]()
                        if pidx == npre - 1:
                            sblock_unit(nxt)
                            side.extend(ssm_units(nxt))
                wb_unit(m)
            done_item(c.pi, "wb0")
            done_item(c.pi, "wb1")
            MK = [("big", k) for k in range(8)]

            def wo_unit(tt):
                for hh in range(2):
                    bank, bkey = next_bank()
                    w = v8(c, "wo%d" % hh)

                    def f(e, w=w, bank=bank):
                        for k in range(8):
                            ins = e.matmul(bank[:TS, :], lhsT=big[:, k, tt * 128:tt * 128 + TS], rhs=w[:, k, :], start=(k == 0), stop=(k == 7))
                        return ins
                    S.add("pe", f, reads=rk(c, "wo%d" % hh) + MK, writes=[bkey])
                    S.add("dve", lambda e, hh=hh, bank=bank: e.tensor_tensor(out=x_sb[tt][:TS, hh * 512:(hh + 1) * 512], in0=bank[:TS, :],
                                                                              in1=x_sb[tt][:TS, hh * 512:(hh + 1) * 512], op=ALU.add),
                          reads=[bkey, ("x", tt)], writes=[("x", tt)])

            his = {}
            for tt in range(NTT + 2):
                if tt < NTT:
                    wo_unit(tt)
                if 2 <= tt and tt - 2 < NTT:
                    norm_tr(his[tt - 2], tt - 2, TS, gfT, "gfT", hTb, "hTb")
                if tt < NTT:
                    his[tt] = norm_chain(x_sb[tt][:TS, :], [("x", tt)], TS)
                    run_side()
            done_item(c.pi, "wo0")
            done_item(c.pi, "wo1")

            HKb = [("hTb", tt) for tt in range(NTT)]
            hrhsb = lambda k: hTb[:, k, :NT]
            for cc in range(11):
                name = "gu%d" % cc
                vgu = ring[:, slot_of(c.pi, name), :].rearrange("p (two kt n) -> p two kt n", two=2, kt=8)
                for j in range(2):
                    bG, kG = fm_group(c, name, j, 8, hrhsb, HKb, view=vgu[:, 0])
                    bU, kU = fm_group(c, name, j, 8, hrhsb, HKb, view=vgu[:, 1])
                    ti = next_tmp()
                    S.add("act", lambda e, bG=bG, ti=ti: e.activation(out=tmp_f[ti][:, :NT], in_=bG[:, :NT], func=AF.Silu), reads=[kG], writes=[("tmpf", ti)])
                    S.add("dve", lambda e, bU=bU, ti=ti, cc=cc, j=j: e.tensor_tensor(out=big[:, 2 * cc + j, :NT], in0=bU[:, :NT], in1=tmp_f[ti][:, :NT], op=ALU.mult),
                          reads=[kU, ("tmpf", ti)], writes=[("big", 2 * cc + j)])
                done_item(c.pi, name)
                run_side()
            AK = [("big", k) for k in range(22)]
            for hh in range(2):
                ws = [v8(c, "d%d" % (3 * hh + p_)) for p_ in range(3)]
                wkeys = rk(c, "d%d" % (3 * hh)) + rk(c, "d%d" % (3 * hh + 1)) + rk(c, "d%d" % (3 * hh + 2))
                for tt in range(NTT):
                    bank, bkey = next_bank()

                    def f(e, tt=tt, ws=ws, bank=bank):
                        for k in range(22):
                            ins = e.matmul(bank[:TS, :], lhsT=big[:, k, tt * 128:tt * 128 + TS], rhs=ws[k // 8][:, k % 8, :], start=(k == 0), stop=(k == 21))
                        return ins
                    S.add("pe", f, reads=wkeys + AK, writes=[bkey])
                    S.add("dve", lambda e, tt=tt, hh=hh, bank=bank: e.tensor_tensor(out=x_sb[tt][:TS, hh * 512:(hh + 1) * 512], in0=bank[:TS, :],
                                                                                     in1=x_sb[tt][:TS, hh * 512:(hh + 1) * 512], op=ALU.add),
                          reads=[bkey, ("x", tt)], writes=[("x", tt)])
                    if hh == 1:
                        xap = x_sb[tt][:TS, :]
                        rstd, sk = rms_rstd(xap, [("x", tt)], TS, next_htm())
                        S.add("act", lambda e, xap=xap, rstd=rstd: e.activation(out=xap, in_=xap, func=AF.Identity, scale=rstd),
                              reads=[("x", tt), sk], writes=[("x", tt)])
                        S.add("pool", lambda e, xap=xap: e.tensor_tensor(out=xap, in0=xap, in1=gfin_t[:TS, :], op=ALU.mult),
                              reads=[("x", tt), "gfin_t"], writes=[("x", tt)])
                        dst = ys if sample else yp[c.tok0 + tt * 128: c.tok0 + (tt + 1) * 128, :]
                        S.add("sp", lambda e, dst=dst, xap=xap: e.dma_start(out=dst, in_=xap), reads=[("x", tt)], dkey=("yo", tt))
                    run_side()
                for p_ in range(3):
                    done_item(c.pi, "d%d" % (3 * hh + p_))
            return side[side_i[0]:]

        ctxs = [Ctx(k, False, k * 512) for k in range(4)] + [Ctx(4, True, 0)]
        ch0, tr0 = pre_units(ctxs[0])
        for a_, b_ in zip(ch0, tr0):
            a_()
            b_()
        sblock_unit(ctxs[0])
        carry = ssm_units(ctxs[0])
        for pi in range(NPASS):
            carry = run_pass(ctxs[pi], ctxs[pi + 1] if pi + 1 < NPASS else None, own=carry)
        assert not carry

        for part in range(2):
            bank, bkey = next_bank()
            S.add("pe", lambda e, part=part, bank=bank: e.transpose(out=bank[:16, 0:128], in_=hst[part][:, :], identity=ident_f[:, :]),
                  reads=[("hst", part, m) for m in range(4)] + ["ident_f"], writes=[bkey])
            S.add("act", lambda e, part=part, bank=bank: e.copy(out=tmp_f[part][:16, 0:128], in_=bank[:16, 0:128]), reads=[bkey], writes=[("tmpf", part)])
            dsto = sre_p if part == 0 else sim_p
            S.add("sp", lambda e, part=part, dsto=dsto: e.dma_start(out=dsto, in_=tmp_f[part][:16, 0:128]), reads=[("tmpf", part)], dkey=("pst", part))

        S.emit()
    return nc


_CACHE = {}


def _host_layouts(inp):
    f = lambda a: np.ascontiguousarray(np.asarray(a, dtype=np.float32))
    L = {}
    L["g_mixT"] = f(inp["norm_mix_g"][0].reshape(8, 128).T)
    L["g_ffnT"] = f(inp["norm_ffn_g"][0].reshape(8, 128).T)
    L["g_fin"] = f(inp["norm_final_g"])
    L["ln_g"] = f(inp["ln_v_g"][0])
    L["ln_b"] = f(inp["ln_v_b"][0])
    L["w_in"] = f(inp["w_in"][0])
    L["w_a"] = f(inp["w_branch_a"][0])
    L["w_b"] = f(inp["w_branch_b"][0])
    L["w_out"] = f(inp["w_out"][0])
    L["w_g"] = f(inp["w_gate_ffn"][0])
    L["w_u"] = f(inp["w_up_ffn"][0])
    L["w_d"] = f(inp["w_down_ffn"][0])
    wsp = np.asarray(inp["w_spatial"][0], dtype=np.float32)
    L["wspT"] = f(wsp.transpose(2, 0, 1))
    s_idx = np.arange(128)
    L["trimask"] = f((s_idx[:, None] <= s_idx[None, :]).astype(np.float32))
    bs = np.asarray(inp["b_spatial"][0], dtype=np.float32)
    L["bsp"] = f(bs.reshape(1, 1024))
    L["wdiag"] = f(np.broadcast_to(wsp[:, 0, 0][None, :], (16, 8)))
    L["bsp_s"] = f(np.broadcast_to(bs[:, 0][:, None], (8, 16)).reshape(1, 128))
    L["ident"] = f(np.eye(128))
    lam_re = np.asarray(inp["ssm_lam_re"][0], dtype=np.float32)
    lam_im = np.asarray(inp["ssm_lam_im"][0], dtype=np.float32)
    log_dt = np.asarray(inp["ssm_log_dt"][0], dtype=np.float32)
    def st_layout(a):
        return f(a.reshape(16, 2, 64).transpose(1, 2, 0).reshape(128, 16))
    L["lamre_s"] = st_layout(lam_re)
    L["lamim_s"] = st_layout(lam_im)
    L["logdt_s"] = st_layout(np.broadcast_to(log_dt[:, None], (32, 64)))
    def b_layout_gp(a):
        x = a.reshape(4, 8, 64)
        x = np.broadcast_to(x[:, :, None, :], (4, 8, 16, 64))
        return f(x.transpose(1, 2, 0, 3).reshape(128, 256))
    L["lamre_b"] = b_layout_gp(lam_re)
    L["lamim_b"] = b_layout_gp(lam_im)
    L["logdt_b"] = b_layout_gp(np.broadcast_to(log_dt[:, None], (32, 64)))
    def b_layout_B(a):
        x = a.reshape(4, 8, 64, 16)
        return f(x.transpose(1, 3, 0, 2).reshape(128, 256))
    L["Bre_b"] = b_layout_B(np.asarray(inp["ssm_b_re"][0], dtype=np.float32))
    L["Bim_b"] = b_layout_B(np.asarray(inp["ssm_b_im"][0], dtype=np.float32))
    row = np.arange(128)
    mB = np.zeros((128, 4, 2), np.float32)
    for ql in range(4):
        for g2 in range(2):
            mB[:, ql, g2] = ((row // 16) == (2 * ql + g2))
    L["maskB"] = f(mB.reshape(128, 8))
    def c_layout(a):
        x = a.reshape(16, 2, 16, 64)
        return f(x.transpose(1, 3, 0, 2).reshape(128, 256))
    L["Cre_l"] = c_layout(np.asarray(inp["ssm_c_re"][0], dtype=np.float32))
    L["Cim_l"] = c_layout(np.asarray(inp["ssm_c_im"][0], dtype=np.float32))
    mC = np.zeros((128, 16, 4, 2), np.float32)
    for q in range(16):
        for g2 in range(2):
            mC[g2 * 64:(g2 + 1) * 64, q, q % 4, g2] = 1.0
    L["maskC"] = f(mC.reshape(128, 128))
    L["dvec"] = f(np.asarray(inp["ssm_d"][0], dtype=np.float32).reshape(4, 128).T)
    return L


def kernel(**inp):
    if "nc" not in _CACHE:
        _CACHE["nc"] = build_program()
    nc = _CACHE["nc"]
    L = _host_layouts(inp)
    xpr = np.asarray(inp["x_prompt"], dtype=np.float32)
    xsa = np.asarray(inp["x_sample"], dtype=np.float32).reshape(128, D)
    sre = np.asarray(inp["state_ssm_re"], dtype=np.float32).reshape(128, 2048)
    sim = np.asarray(inp["state_ssm_im"], dtype=np.float32).reshape(128, 2048)
    in_maps = []
    for c in range(8):
        m = dict(L)
        m["xp"] = np.ascontiguousarray(xpr[c])
        m["xs"] = np.ascontiguousarray(xsa[16 * c:16 * c + 16])
        m["h0re"] = np.ascontiguousarray(sre[16 * c:16 * c + 16])
        m["h0im"] = np.ascontiguousarray(sim[16 * c:16 * c + 16])
        in_maps.append(m)
    res = run_bass_kernel_spmd(nc, in_maps, core_ids=list(range(8)))
    R = res.results
    y_prompt = np.stack([R[c]["yp"] for c in range(8)]).astype(np.float32)
    y_sample = np.concatenate([R[c]["ys"] for c in range(8)]).reshape(128, 1, D).astype(np.float32)
    p_re = np.stack([R[c]["sre_p"].reshape(32, 64) for c in range(8)])[None].astype(np.float32)
    p_im = np.stack([R[c]["sim_p"].reshape(32, 64) for c in range(8)])[None].astype(np.float32)
    s_re = np.concatenate([R[c]["sre_s"] for c in range(8)]).reshape(1, 128, 32, 64).astype(np.float32)
    s_im = np.concatenate([R[c]["sim_s"] for c in range(8)]).reshape(1, 128, 32, 64).astype(np.float32)
    v_s = np.concatenate([R[c]["v_s"] for c in range(8)]).reshape(1, 128, 1, D).astype(np.float32)
    return (y_prompt, y_sample, p_re, p_im, s_re, s_im, v_s)
```

```python
import contextlib
import numpy as np
import concourse.bass as bass
import concourse.mybir as mybir
from concourse.bass_utils import run_bass_kernel_spmd

F32 = mybir.dt.float32
BF16 = mybir.dt.bfloat16
AF = mybir.ActivationFunctionType
ALU = mybir.AluOpType

D = 1024
SEQ = 2048
NSAMP = 16
DFF = 2816
EPS = 1e-6
NSLOT = 5
NPASS = 5


class Sched:
    ENGS = ["pe", "act", "dve", "pool", "sp"]

    def __init__(self, nc):
        self.nc = nc
        self.ops = []
        self.last_w = {}
        self.readers = {}

    def add(self, eng, fn, reads=(), writes=(), dkey=None):
        i = len(self.ops)
        deps = set()
        for k in list(reads) + list(writes):
            if k in self.last_w:
                deps.add(self.last_w[k])
        for k in writes:
            deps.update(self.readers.get(k, ()))
        self.ops.append(dict(eng=eng, fn=fn, deps=deps, dkey=dkey, sig=False))
        for k in reads:
            self.readers.setdefault(k, []).append(i)
        for k in writes:
            self.last_w[k] = i
            self.readers[k] = []
        return i

    def emit(self, final_wait_eng="sp"):
        nc = self.nc
        ops = self.ops
        for op in ops:
            for d in op["deps"]:
                dop = ops[d]
                if dop["dkey"] is not None:
                    continue
                if dop["eng"] == op["eng"] and dop["eng"] == "pe":
                    continue
                dop["sig"] = True
        eng_cnt = {e: 0 for e in self.ENGS}
        dma_cnt = {}
        for op in ops:
            if op["dkey"] is not None:
                dma_cnt[op["dkey"]] = dma_cnt.get(op["dkey"], 0) + 1
                op["tick"] = 16 * dma_cnt[op["dkey"]]
            elif op["sig"]:
                eng_cnt[op["eng"]] += 1
                op["tick"] = eng_cnt[op["eng"]]
        with contextlib.ExitStack() as st:
            esem = {e: st.enter_context(nc.semaphore("s_" + e)) for e in ["pe", "act", "dve", "pool"]}
            dsem = {k: st.enter_context(nc.semaphore("d_%d" % n)) for n, k in enumerate(dma_cnt)}
            block = st.enter_context(nc.Block())

            def stream(ename):
                def body(eng):
                    seen = {}
                    for op in ops:
                        if op["eng"] != ename:
                            continue
                        need = {}
                        for d in op["deps"]:
                            dop = ops[d]
                            if dop["dkey"] is not None:
                                s = dsem[dop["dkey"]]
                            else:
                                if not dop["sig"]:
                                    continue
                                if dop["eng"] == ename and ename == "pe":
                                    continue
                                s = esem[dop["eng"]]
                            key = id(s)
                            if dop["tick"] > need.get(key, (None, 0))[1]:
                                need[key] = (s, dop["tick"])
                        for key, (s, v) in need.items():
                            if v > seen.get(key, 0):
                                eng.wait_ge(s, v)
                                seen[key] = v
                        ins = op["fn"](eng)
                        if op["dkey"] is not None:
                            ins.then_inc(dsem[op["dkey"]], 16)
                        elif op["sig"]:
                            ins.then_inc(esem[ename], 1)
                    if ename == final_wait_eng:
                        for k, c in dma_cnt.items():
                            eng.wait_ge(dsem[k], 16 * c)

                return body

            block.tensor(stream("pe"))
            block.scalar(stream("act"))
            block.vector(stream("dve"))
            block.gpsimd(stream("pool"))
            block.sync(stream("sp"))


def build_program():
    nc = bass.Bass("TRN2", target_bir_lowering=False)

    def din(name, shape):
        return nc.dram_tensor(name, list(shape), F32, kind="ExternalInput").ap()

    def dout(name, shape):
        return nc.dram_tensor(name, list(shape), F32, kind="ExternalOutput").ap()

    xp = din("xp", [SEQ, D])
    xs = din("xs", [NSAMP, D])
    h0re = din("h0re", [NSAMP, 2048])
    h0im = din("h0im", [NSAMP, 2048])
    g_mixT = din("g_mixT", [128, 8])
    g_ffnT = din("g_ffnT", [128, 8])
    g_fin = din("g_fin", [D])
    ln_g = din("ln_g", [D])
    ln_b = din("ln_b", [D])
    w_in = din("w_in", [D, 4608])
    w_a = din("w_a", [D, D])
    w_b = din("w_b", [512, 2048])
    w_out = din("w_out", [D, D])
    w_g = din("w_g", [D, DFF])
    w_u = din("w_u", [D, DFF])
    w_d = din("w_d", [DFF, D])
    wspT = din("wspT", [128, 8, 128])
    trimask = din("trimask", [128, 128])
    bsp = din("bsp", [1, 1024])
    wdiag = din("wdiag", [16, 8])
    bsp_s = din("bsp_s", [1, 128])
    ident = din("ident", [128, 128])
    lamre_s = din("lamre_s", [128, 16])
    lamim_s = din("lamim_s", [128, 16])
    logdt_s = din("logdt_s", [128, 16])
    lamre_b = din("lamre_b", [128, 256])
    lamim_b = din("lamim_b", [128, 256])
    logdt_b = din("logdt_b", [128, 256])
    Bre_b = din("Bre_b", [128, 256])
    Bim_b = din("Bim_b", [128, 256])
    maskB = din("maskB", [128, 8])
    Cre_l = din("Cre_l", [128, 256])
    Cim_l = din("Cim_l", [128, 256])
    maskC = din("maskC", [128, 128])
    dvec = din("dvec", [128, 4])

    yp = dout("yp", [SEQ, D])
    ys = dout("ys", [NSAMP, D])
    sre_p = dout("sre_p", [16, 128])
    sim_p = dout("sim_p", [16, 128])
    sre_s = dout("sre_s", [NSAMP, 2048])
    sim_s = dout("sim_s", [NSAMP, 2048])
    v_s = dout("v_s", [NSAMP, D])

    with contextlib.ExitStack() as st:
        def sb(name, shape, dt=F32):
            return st.enter_context(nc.sbuf_tensor(name, list(shape), dt))

        def ps(name, shape, dt=F32):
            return st.enter_context(nc.psum_tensor(name, list(shape), dt))

        ring = sb("ring", [128, NSLOT, 4096], BF16)
        x_sb = [sb("x%d" % i, [128, D]) for i in range(4)]
        h_tm2 = [sb("h_tm%d" % i, [128, D], BF16) for i in range(2)]
        hTa = sb("hTa", [128, 8, 512], BF16)
        hTb = sb("hTb", [128, 8, 512], BF16)
        big = sb("big", [128, 22, 512], BF16)
        vg = sb("vg", [128, D])
        v_ln = [sb("vln%d" % i, [128, D], BF16) for i in range(2)]
        sT = sb("sT", [128, 4, 512], BF16)
        yT = sb("yT", [128, 4, 512], BF16)
        tmp_f = [sb("tmpf%d" % i, [128, 512]) for i in range(2)]
        tmp_b = [sb("tmpb%d" % i, [128, 512], BF16) for i in range(2)]
        stat = [sb("stat%d" % i, [128, 16]) for i in range(4)]
        tSall = sb("tSall", [128, 12, 512])
        tS = [tSall[:, i, :].rearrange("p (a b) -> p a b", a=4) for i in range(12)]
        h_bf = [sb("hbf%d" % i, [128, 2, 4, 128], BF16) for i in range(2)]
        hst = [sb("hst_re", [128, 16]), sb("hst_im", [128, 16])]
        ident_f = sb("ident_f", [128, 128])
        ident_b = sb("ident_b", [128, 128], BF16)
        gmT = sb("gmT", [128, 8])
        gfT = sb("gfT", [128, 8])
        gfin_t = sb("gfin_t", [128, D])
        lng_t = sb("lng_t", [128, D])
        lnb_t = sb("lnb_t", [128, D])
        WspT = sb("WspT", [128, 8, 128], BF16)
        WspT_s = sb("WspT_s", [16, 8, 16], BF16)
        bsp_bf = sb("bsp_bf", [1, 8, 128], BF16)
        bsps_bf = sb("bsps_bf", [1, 8, 16], BF16)
        ones_bf = sb("ones_bf", [1, 128], BF16)
        cosT = sb("cosT", [128, 16, 128])
        sinT = sb("sinT", [128, 16, 128])
        rho_s = sb("rho_s", [128, 16])
        cosB = sb("cosB", [128, 16, 128], BF16)
        sinB = sb("sinB", [128, 16, 128], BF16)
        sst = sb("sst", [128, 2, 16])
        lb_re = sb("lb_re", [128, 16])
        lb_im = sb("lb_im", [128, 16])
        lhsT_BU = sb("lhsT_BU", [128, 4, 4, 2, 128], BF16)
        lhsT_C = sb("lhsT_C", [128, 16, 2, 128], BF16)
        diagD = sb("diagD", [128, 4, 128], BF16)
        h0T = [tSall[:, 6 + i, 0:256].rearrange("p (q t) -> p q t", q=16) for i in range(2)]
        hs = [tSall[:, 8 + i, 0:256].rearrange("p (q t) -> p q t", q=16) for i in range(2)]
        hs_bf = tmp_b[1][:, :].rearrange("p (q a t) -> p q a t", q=16, a=2)
        scrA = vg[:, :].rearrange("p (g t) -> p g t", g=8)
        scrB = tmp_f[1][:, :]
        scrC = x_sb[0][:, 0:640]
        gfin_scr2 = tSall[:, 4:6, :].rearrange("p a b -> p (a b)")
        pT = ps("pT", [128, 8, 128], BF16)
        gbank = [ps("gb%d" % i, [128, 512]) for i in range(4)]
        bRe = ps("bRe", [128, 512])
        bIm = ps("bIm", [128, 512])
        bY = ps("bY", [128, 512])

        S = Sched(nc)
        gb_ctr = [0]

        def next_bank():
            i = gb_ctr[0] % 4
            gb_ctr[0] += 1
            return gbank[i], ("gb", i)

        def ld(dst, src, key, eng="sp"):
            S.add(eng, lambda e: e.dma_start(out=dst, in_=src), writes=[key], dkey=("setup", key))

        ld(ident_f[:], ident, "ident_f")
        ld(ident_b[:], ident, "ident_b", eng="pool")
        ld(gmT[:], g_mixT, "gmT")
        ld(gfT[:], g_ffnT, "gfT")
        ld(gfin_t[:], g_fin.partition_broadcast(128), "gfin_t")
        ld(lng_t[:], ln_g.partition_broadcast(128), "lng_t")
        ld(lnb_t[:], ln_b.partition_broadcast(128), "lnb_t")
        ld(bsp_bf[:], bsp.rearrange("o (g t) -> o g t", g=8), "bsp_bf", eng="pool")
        ld(bsps_bf[:], bsp_s.rearrange("o (g t) -> o g t", g=8), "bsps_bf", eng="pool")
        S.add("dve", lambda e: e.memset(ones_bf[:], 1.0), writes=["ones_bf"])
        S.add("dve", lambda e: e.memset(hst[0][:], 0.0), writes=[("hst", 0, m) for m in range(4)])
        S.add("dve", lambda e: e.memset(hst[1][:], 0.0), writes=[("hst", 1, m) for m in range(4)])

        S.add("sp", lambda e: e.dma_start(out=scrA, in_=wspT), writes=[("vg", 0), ("vg", 1)], dkey=("setup", "scrA"))
        ld(scrB[:, 0:128], trimask, "trimask")
        S.add("dve", lambda e: e.tensor_tensor(out=WspT[:], in0=scrA, in1=scrB[:, 0:128].unsqueeze(1).broadcast_to([128, 8, 128]), op=ALU.mult),
              reads=[("vg", 0), ("vg", 1), "trimask"], writes=["WspT"])
        ld(scrB[:16, 128:136], wdiag, "wdiag")
        S.add("dve", lambda e: e.tensor_tensor(out=WspT_s[:], in0=ident_f[:16, :16].unsqueeze(1).broadcast_to([16, 8, 16]),
                                               in1=scrB[:16, 128:136].unsqueeze(2).broadcast_to([16, 8, 16]), op=ALU.mult),
              reads=["ident_f", "wdiag"], writes=["WspT_s"])
        ld(scrB[:, 136:140], dvec, "dvec")
        for m in range(4):
            S.add("dve", lambda e, m=m: e.tensor_scalar(out=diagD[:, m, :], in0=ident_f[:], scalar1=scrB[:, 136 + m:137 + m],
                                                        scalar2=None, op0=ALU.mult),
                  reads=["ident_f", "dvec"], writes=[("diagD", m)])

        HALF_PI = float(np.pi / 2)

        def lam_tables(pfx, lamre_ap, lamim_ap, logdt_ap, n, bufs, XR=()):
            lr, li, dt, a, c, s, t1, t2 = bufs[:8]
            k = lambda i: "%s_%d" % (pfx, i)
            S.add("sp", lambda e: e.dma_start(out=lr, in_=lamre_ap), reads=list(XR), writes=[k(0)], dkey=("setup", k(0)))
            S.add("sp", lambda e: e.dma_start(out=li, in_=lamim_ap), reads=list(XR), writes=[k(1)], dkey=("setup", k(1)))
            S.add("sp", lambda e: e.dma_start(out=dt, in_=logdt_ap), reads=list(XR), writes=[k(2)], dkey=("setup", k(2)))
            S.add("act", lambda e: e.activation(out=dt, in_=dt, func=AF.Exp), reads=list(XR) + [k(2)], writes=[k(2)])
            S.add("dve", lambda e: e.tensor_tensor(out=a, in0=lr, in1=dt, op=ALU.mult), reads=list(XR) + [k(0), k(2)], writes=[k(3)])
            S.add("act", lambda e: e.activation(out=a, in_=a, func=AF.Exp), reads=list(XR) + [k(3)], writes=[k(3)])
            S.add("dve", lambda e: e.scalar_tensor_tensor(out=t1, in0=li, scalar=1.0 / 16, in1=dt, op0=ALU.mult, op1=ALU.mult),
                  reads=list(XR) + [k(1), k(2)], writes=[k(6)])
            S.add("dve", lambda e: e.tensor_scalar(out=t2, in0=t1, scalar1=HALF_PI, scalar2=None, op0=ALU.add),
                  reads=list(XR) + [k(6)], writes=[k(7)])
            S.add("act", lambda e: e.activation(out=s, in_=t1, func=AF.Sin), reads=list(XR) + [k(6)], writes=[k(5)])
            S.add("act", lambda e: e.activation(out=c, in_=t2, func=AF.Sin), reads=list(XR) + [k(7)], writes=[k(4)])
            for _ in range(4):
                S.add("dve", lambda e: e.tensor_tensor(out=t1, in0=c, in1=c, op=ALU.mult), reads=list(XR) + [k(4)], writes=[k(6)])
                S.add("dve", lambda e: e.tensor_tensor(out=t2, in0=s, in1=s, op=ALU.mult), reads=list(XR) + [k(5)], writes=[k(7)])
                S.add("dve", lambda e: e.scalar_tensor_tensor(out=s, in0=c, scalar=2.0, in1=s, op0=ALU.mult, op1=ALU.mult),
                      reads=list(XR) + [k(4), k(5)], writes=[k(5)])
                S.add("dve", lambda e: e.tensor_tensor(out=c, in0=t1, in1=t2, op=ALU.subtract), reads=list(XR) + [k(6), k(7)], writes=[k(4)])
            return dict(rho=(a, k(3)), c=(c, k(4)), s=(s, k(5)), lr=(lr, k(0)), li=(li, k(1)), t1=(t1, k(6)), t2=(t2, k(7)))

        scr_s = [scrB[:, 144 + 16 * i:160 + 16 * i] for i in range(8)]
        Ts = lam_tables("ls", lamre_s, lamim_s, logdt_s, 16, scr_s)
        S.add("pool", lambda e: e.tensor_copy(out=rho_s[:], in_=Ts["rho"][0]), reads=[Ts["rho"][1]], writes=["rho_s"])
        S.add("dve", lambda e: e.tensor_tensor(out=lb_re[:], in0=Ts["rho"][0], in1=Ts["c"][0], op=ALU.mult),
              reads=[Ts["rho"][1], Ts["c"][1]], writes=["lb_re"])
        S.add("dve", lambda e: e.tensor_tensor(out=lb_im[:], in0=Ts["rho"][0], in1=Ts["s"][0], op=ALU.mult),
              reads=[Ts["rho"][1], Ts["s"][1]], writes=["lb_im"])
        Ec = scrB[:, 272:288]
        Es = scrB[:, 288:304]
        Et1 = scrB[:, 304:320]
        Et2 = scrB[:, 320:336]
        S.add("dve", lambda e: e.tensor_copy(out=Ec, in_=Ts["c"][0]), reads=[Ts["c"][1]], writes=["Ec"])
        S.add("dve", lambda e: e.tensor_copy(out=Es, in_=Ts["s"][0]), reads=[Ts["s"][1]], writes=["Es"])
        S.add("dve", lambda e: e.tensor_copy(out=cosT[:, :, 0], in_=Ec), reads=["Ec"], writes=["cosT"])
        S.add("dve", lambda e: e.tensor_copy(out=sinT[:, :, 0], in_=Es), reads=["Es"], writes=["sinT"])
        rtA = vg[:, :].rearrange("p (q j) -> p q j", q=16)
        rtB = x_sb[0][:, :].rearrange("p (q j) -> p q j", q=16)
        for kk in range(7):
            step = 1 << kk
            ecb = Ec.unsqueeze(2).broadcast_to([128, 16, step])
            esb = Es.unsqueeze(2).broadcast_to([128, 16, step])
            srcc = cosT[:, :, 0:step]
            srcs = sinT[:, :, 0:step]
            dstc = cosT[:, :, step:2 * step]
            dsts = sinT[:, :, step:2 * step]
            A = rtA[:, :, 0:step]
            Bq = rtB[:, :, 0:step]
            S.add("dve", lambda e, A=A, srcc=srcc, ecb=ecb: e.tensor_tensor(out=A, in0=srcc, in1=ecb, op=ALU.mult),
                  reads=["cosT", "Ec"], writes=[("vg", 0), ("vg", 1)])
            S.add("dve", lambda e, Bq=Bq, srcs=srcs, esb=esb: e.tensor_tensor(out=Bq, in0=srcs, in1=esb, op=ALU.mult),
                  reads=["sinT", "Es"], writes=[("x", 0)])
            S.add("dve", lambda e, A=A, Bq=Bq, dstc=dstc: e.tensor_tensor(out=dstc, in0=A, in1=Bq, op=ALU.subtract),
                  reads=[("vg", 0), ("vg", 1), ("x", 0)], writes=["cosT"])
            S.add("dve", lambda e, A=A, srcc=srcc, esb=esb: e.tensor_tensor(out=A, in0=srcc, in1=esb, op=ALU.mult),
                  reads=["cosT", "Es"], writes=[("vg", 0), ("vg", 1)])
            S.add("dve", lambda e, Bq=Bq, srcs=srcs, ecb=ecb: e.tensor_tensor(out=Bq, in0=srcs, in1=ecb, op=ALU.mult),
                  reads=["sinT", "Ec"], writes=[("x", 0)])
            S.add("dve", lambda e, A=A, Bq=Bq, dsts=dsts: e.tensor_tensor(out=dsts, in0=A, in1=Bq, op=ALU.add),
                  reads=[("vg", 0), ("vg", 1), ("x", 0)], writes=["sinT"])
            if kk < 6:
                S.add("dve", lambda e: e.tensor_tensor(out=Et1, in0=Ec, in1=Ec, op=ALU.mult), reads=["Ec"], writes=["Et1"])
                S.add("dve", lambda e: e.tensor_tensor(out=Et2, in0=Es, in1=Es, op=ALU.mult), reads=["Es"], writes=["Et2"])
                S.add("dve", lambda e: e.scalar_tensor_tensor(out=Es, in0=Ec, scalar=2.0, in1=Es, op0=ALU.mult, op1=ALU.mult),
                      reads=["Ec", "Es"], writes=["Es"])
                S.add("dve", lambda e: e.tensor_tensor(out=Ec, in0=Et1, in1=Et2, op=ALU.subtract), reads=["Et1", "Et2"], writes=["Ec"])

        S.add("act", lambda e: e.copy(out=cosB[:], in_=cosT[:]), reads=["cosT"], writes=["cosB"])
        S.add("act", lambda e: e.copy(out=sinB[:], in_=sinT[:]), reads=["sinT"], writes=["sinB"])
        scr_b = [x_sb[1][:, 256 * i:256 * i + 256] for i in range(4)] + [x_sb[2][:, 256 * i:256 * i + 256] for i in range(4)]
        XR = [("x", 1), ("x", 2), ("x", 3)]
        Tb = lam_tables("lb", lamre_b, lamim_b, logdt_b, 256, scr_b, XR)
        sc3 = [x_sb[3][:, 256 * i:256 * i + 256] for i in range(4)]
        bre_t, bim_t, nre, nim = sc3
        S.add("sp", lambda e: e.dma_start(out=bre_t, in_=Bre_b), reads=XR, writes=["bre_t"], dkey=("setup", "bre_t"))
        S.add("sp", lambda e: e.dma_start(out=bim_t, in_=Bim_b), reads=XR, writes=["bim_t"], dkey=("setup", "bim_t"))
        rho_b, c_b, s_b = Tb["rho"], Tb["c"], Tb["s"]
        lr_b, li_b = Tb["lr"], Tb["li"]
        t1_b, t2_b = Tb["t1"], Tb["t2"]
        S.add("dve", lambda e: e.tensor_tensor(out=t1_b[0], in0=rho_b[0], in1=c_b[0], op=ALU.mult), reads=XR + [rho_b[1], c_b[1]], writes=[t1_b[1]])
        S.add("dve", lambda e: e.tensor_scalar(out=t1_b[0], in0=t1_b[0], scalar1=-1.0, scalar2=None, op0=ALU.add), reads=XR + [t1_b[1]], writes=[t1_b[1]])
        S.add("dve", lambda e: e.tensor_tensor(out=t2_b[0], in0=rho_b[0], in1=s_b[0], op=ALU.mult), reads=XR + [rho_b[1], s_b[1]], writes=[t2_b[1]])
        S.add("dve", lambda e: e.tensor_tensor(out=nre, in0=t1_b[0], in1=lr_b[0], op=ALU.mult), reads=XR + [t1_b[1], lr_b[1]], writes=["nre"])
        S.add("dve", lambda e: e.tensor_tensor(out=c_b[0], in0=t2_b[0], in1=li_b[0], op=ALU.mult), reads=XR + [t2_b[1], li_b[1], c_b[1]], writes=[c_b[1]])
        S.add("dve", lambda e: e.tensor_tensor(out=nre, in0=nre, in1=c_b[0], op=ALU.add), reads=XR + ["nre", c_b[1]], writes=["nre"])
        S.add("dve", lambda e: e.tensor_tensor(out=nim, in0=t2_b[0], in1=lr_b[0], op=ALU.mult), reads=XR + [t2_b[1], lr_b[1]], writes=["nim"])
        S.add("dve", lambda e: e.tensor_tensor(out=c_b[0], in0=t1_b[0], in1=li_b[0], op=ALU.mult), reads=XR + [t1_b[1], li_b[1], c_b[1]], writes=[c_b[1]])
        S.add("dve", lambda e: e.tensor_tensor(out=nim, in0=nim, in1=c_b[0], op=ALU.subtract), reads=XR + ["nim", c_b[1]], writes=["nim"])
        S.add("dve", lambda e: e.tensor_tensor(out=c_b[0], in0=lr_b[0], in1=lr_b[0], op=ALU.mult), reads=XR + [lr_b[1], c_b[1]], writes=[c_b[1]])
        S.add("dve", lambda e: e.tensor_tensor(out=s_b[0], in0=li_b[0], in1=li_b[0], op=ALU.mult), reads=XR + [li_b[1], s_b[1]], writes=[s_b[1]])
        S.add("dve", lambda e: e.tensor_tensor(out=c_b[0], in0=c_b[0], in1=s_b[0], op=ALU.add), reads=XR + [c_b[1], s_b[1]], writes=[c_b[1]])
        S.add("dve", lambda e: e.reciprocal(out=c_b[0], in_=c_b[0]), reads=XR + [c_b[1]], writes=[c_b[1]])
        S.add("dve", lambda e: e.tensor_tensor(out=nre, in0=nre, in1=c_b[0], op=ALU.mult), reads=XR + ["nre", c_b[1]], writes=["nre"])
        S.add("dve", lambda e: e.tensor_tensor(out=nim, in0=nim, in1=c_b[0], op=ALU.mult), reads=XR + ["nim", c_b[1]], writes=["nim"])
        S.add("dve", lambda e: e.tensor_tensor(out=t1_b[0], in0=nre, in1=bre_t, op=ALU.mult), reads=XR + ["nre", "bre_t", t1_b[1]], writes=[t1_b[1]])
        S.add("dve", lambda e: e.tensor_tensor(out=s_b[0], in0=nim, in1=bim_t, op=ALU.mult), reads=XR + ["nim", "bim_t", s_b[1]], writes=[s_b[1]])
        S.add("dve", lambda e: e.tensor_tensor(out=t1_b[0], in0=t1_b[0], in1=s_b[0], op=ALU.subtract), reads=XR + [t1_b[1], s_b[1]], writes=[t1_b[1]])
        S.add("dve", lambda e: e.tensor_tensor(out=t2_b[0], in0=nre, in1=bim_t, op=ALU.mult), reads=XR + ["nre", "bim_t", t2_b[1]], writes=[t2_b[1]])
        S.add("dve", lambda e: e.tensor_tensor(out=s_b[0], in0=nim, in1=bre_t, op=ALU.mult), reads=XR + ["nim", "bre_t", s_b[1]], writes=[s_b[1]])
        S.add("dve", lambda e: e.tensor_tensor(out=t2_b[0], in0=t2_b[0], in1=s_b[0], op=ALU.add), reads=XR + [t2_b[1], s_b[1]], writes=[t2_b[1]])
        mB = scrB[:, 336:344]
        ld(mB, maskB, "mB")
        for kt in range(4):
            for part in range(2):
                src = (t1_b if part == 0 else t2_b)
                def f(e, kt=kt, part=part, src=src):
                    return e.tensor_tensor(
                        out=lhsT_BU[:, kt, :, part, :].rearrange("p q (g x) -> p q g x", g=2),
                        in0=src[0][:, kt * 64:(kt + 1) * 64].unsqueeze(1).unsqueeze(1).broadcast_to([128, 4, 2, 64]),
                        in1=mB.rearrange("p (q g) -> p q g", g=2).unsqueeze(3).broadcast_to([128, 4, 2, 64]),
                        op=ALU.mult)
                S.add("dve", f, reads=XR + [src[1], "mB"], writes=[("lhsT_BU", kt, part)])
        cl_re = scrC[:, 0:256]
        cl_im = scrC[:, 256:512]
        mC = scrC[:, 512:640]
        S.add("sp", lambda e: e.dma_start(out=cl_re, in_=Cre_l), writes=[("x", 0)], dkey=("setup", "cl_re"))
        S.add("sp", lambda e: e.dma_start(out=cl_im, in_=Cim_l), reads=[("x", 0)], writes=["cl_im"], dkey=("setup", "cl_im"))
        S.add("sp", lambda e: e.dma_start(out=mC, in_=maskC), reads=[("x", 0)], writes=["mC"], dkey=("setup", "mC"))
        S.add("dve", lambda e: e.tensor_scalar(out=cl_im, in0=cl_im, scalar1=-1.0, scalar2=None, op0=ALU.mult), reads=["cl_im", ("x", 0)], writes=["cl_im"])
        for part in range(2):
            src, skey = (cl_re, ("x", 0)) if part == 0 else (cl_im, "cl_im")
            def f(e, part=part, src=src):
                return e.tensor_tensor(
                    out=lhsT_C[:, :, part, :].rearrange("p q (j h) -> p q j h", h=16),
                    in0=src.rearrange("p (q h) -> p q h", h=16).unsqueeze(2).broadcast_to([128, 16, 8, 16]),
                    in1=mC.rearrange("p (q j) -> p q j", j=8).unsqueeze(3).broadcast_to([128, 16, 8, 16]),
                    op=ALU.mult)
            S.add("dve", f, reads=[skey, "mC", ("x", 0)], writes=[("lhsT_C", part)])
        LBU_KEYS = [("lhsT_BU", kt, part) for kt in range(4) for part in range(2)]
        LC_KEYS = [("lhsT_C", 0), ("lhsT_C", 1)]
        DD_KEYS = [("diagD", m) for m in range(4)]

        SCRB_KEYS = ["trimask", "wdiag", "dvec", "mB", "Ec", "Es", "Et1", "Et2"] + ["ls_%d" % i for i in range(8)]
        S.add("dve", lambda e: e.memset(tmp_f[1][0:1, 0:1], 0.0), writes=SCRB_KEYS + [("tmpf", 1)])

        ITEM_SEQ = []
        for pi_ in range(NPASS):
            if pi_ == 0:
                ITEM_SEQ.append((0, "s"))
            ITEM_SEQ += [(pi_, n_) for n_ in ["u0", "u1", "v0", "v1", "ga0", "ga1", "wa0", "wa1", "gb0", "gb1"]]
            if pi_ + 1 < NPASS:
                ITEM_SEQ.append((pi_ + 1, "s"))
            ITEM_SEQ += [(pi_, n_) for n_ in ["wb0", "wb1", "wo0", "wo1"]]
            ITEM_SEQ += [(pi_, "gu%d" % c_) for c_ in range(11)]
            ITEM_SEQ += [(pi_, "d%d" % j_) for j_ in range(6)]
        GIDX = {k_: i_ for i_, k_ in enumerate(ITEM_SEQ)}

        def slot_keys(s_):
            return [("ring", s_, 0), ("ring", s_, 1)]

        def item_dmas(name, s_):
            sl = ring[:, s_, :]
            v8_ = sl.rearrange("p (kt n) -> p kt n", kt=8)
            WIN = {"s": 2048, "u0": 0, "u1": 512, "v0": 1024, "v1": 1536, "ga0": 2560, "ga1": 3072, "gb0": 3584, "gb1": 4096}
            if name in WIN:
                c0 = WIN[name]
                return [(v8_, w_in[:, c0:c0 + 512].rearrange("(kt p) n -> p kt n", p=128), None)]
            if name.startswith("wa"):
                c0 = int(name[2:]) * 512
                return [(v8_, w_a[:, c0:c0 + 512].rearrange("(kt p) n -> p kt n", p=128), None)]
            if name.startswith("wb"):
                c0 = int(name[2:]) * 1024
                return [(sl.rearrange("p (kt n) -> p kt n", kt=4), w_b[:, c0:c0 + 1024].rearrange("(kt p) n -> p kt n", p=128), None)]
            if name.startswith("wo"):
                c0 = int(name[2:]) * 512
                return [(v8_, w_out[:, c0:c0 + 512].rearrange("(kt p) n -> p kt n", p=128), None)]
            if name.startswith("gu"):
                c0 = int(name[2:]) * 256
                v_ = sl.rearrange("p (two kt n) -> p two kt n", two=2, kt=8)
                return [(v_[:, 0], w_g[:, c0:c0 + 256].rearrange("(kt p) n -> p kt n", p=128), 0),
                        (v_[:, 1], w_u[:, c0:c0 + 256].rearrange("(kt p) n -> p kt n", p=128), 1)]
            j_ = int(name[1:])
            hh_, part_ = j_ // 3, j_ % 3
            nk = 8 if part_ < 2 else 6
            r0 = part_ * 1024
            return [(v8_[:, 0:nk, :], w_d[r0:r0 + nk * 128, hh_ * 512:(hh_ + 1) * 512].rearrange("(kt p) n -> p kt n", p=128), None)]

        issued = [0]
        consumed = [0]

        NAME_IDX = {}
        for (_p, _n) in ITEM_SEQ:
            if _n not in NAME_IDX:
                NAME_IDX[_n] = len(NAME_IDX)
        wscr = nc.dram_tensor("wscr", [len(NAME_IDX), 128, 4096], BF16).ap()
        first_seen = set()

        def issue_next():
            g = issued[0]
            if g >= len(ITEM_SEQ):
                return
            issued[0] += 1
            s_ = g % NSLOT
            name = ITEM_SEQ[g][1]
            idx = NAME_IDX[name]
            if name not in first_seen:
                first_seen.add(name)
                for dst, src, half in item_dmas(name, s_):
                    if half is None:
                        wk = slot_keys(s_)
                        dk = ("ring", s_, 0)
                    else:
                        wk = [("ring", s_, half)]
                        dk = ("ring", s_, half)
                    S.add("pool", lambda e, dst=dst, src=src: e.dma_start(out=dst, in_=src), writes=wk, dkey=dk)
                S.add("sp", lambda e, s_=s_, idx=idx: e.dma_start(out=wscr[idx], in_=ring[:, s_, :]),
                      reads=slot_keys(s_), writes=[("scr", idx)], dkey=("wbk", s_))
            else:
                S.add("sp", lambda e, s_=s_, idx=idx: e.dma_start(out=ring[:, s_, :], in_=wscr[idx]),
                      reads=[("scr", idx)], writes=slot_keys(s_), dkey=("ringhw", s_))

        def done_item(pi_, name):
            assert GIDX[(pi_, name)] == consumed[0], (pi_, name, consumed[0])
            consumed[0] += 1
            issue_next()

        def slot_of(pi_, name):
            g = GIDX[(pi_, name)]
            assert consumed[0] <= g < issued[0], (pi_, name, g, consumed[0], issued[0])
            return g % NSLOT

        for _ in range(NSLOT):
            issue_next()

        stat_ctr = [0]

        def next_stat():
            i = stat_ctr[0] % 4
            stat_ctr[0] += 1
            return stat[i], ("stat", i)

        tmp_ctr = [0]

        def next_tmp():
            i = tmp_ctr[0] % 2
            tmp_ctr[0] += 1
            return i

        htm_ctr = [0]

        def next_htm():
            i = htm_ctr[0] % 2
            htm_ctr[0] += 1
            return i

        def rms_rstd(xap, xkeys, TS, hi):
            stt, sk = next_stat()
            S.add("act", lambda e: e.activation(out=h_tm2[hi][:TS, :], in_=xap, func=AF.Square, accum_out=stt[:TS, 0:1]),
                  reads=list(xkeys), writes=[("h_tm", hi), sk])
            S.add("act", lambda e: e.activation(out=stt[:TS, 2:3], in_=stt[:TS, 0:1], func=AF.Ln, scale=1.0 / D, bias=EPS), reads=[sk], writes=[sk])
            S.add("act", lambda e: e.activation(out=stt[:TS, 3:4], in_=stt[:TS, 2:3], func=AF.Exp, scale=-0.5), reads=[sk], writes=[sk])
            return stt[:TS, 3:4], sk

        def norm_chain(xap, xkeys, TS):
            hi = next_htm()
            rstd, sk = rms_rstd(xap, xkeys, TS, hi)
            S.add("act", lambda e: e.activation(out=h_tm2[hi][:TS, :], in_=xap, func=AF.Identity, scale=rstd),
                  reads=list(xkeys) + [sk], writes=[("h_tm", hi)])
            return hi

        def norm_tr(hi, tt, TS, gT, gkey, dstT, dkeyname):
            def tr(e):
                for kt in range(8):
                    ins = e.transpose(out=pT[:, kt, :TS], in_=h_tm2[hi][:TS, kt * 128:(kt + 1) * 128], identity=ident_b[:TS, :TS])
                return ins
            S.add("pe", tr, reads=[("h_tm", hi), "ident_b"], writes=["pT"])
            S.add("dve", lambda e: e.tensor_tensor(out=dstT[:, :, tt * 128:tt * 128 + TS], in0=pT[:, :, :TS],
                                                   in1=gT[:, :].unsqueeze(2).broadcast_to([128, 8, TS]), op=ALU.mult),
                  reads=["pT", gkey], writes=[(dkeyname, tt)])

        class Ctx:
            def __init__(self, pi, sample, tok0):
                self.pi = pi
                self.sample = sample
                self.NT = 16 if sample else 512
                self.TS = 16 if sample else 128
                self.NTT = 1 if sample else 4
                self.tok0 = tok0

        def v8(c, name):
            return ring[:, slot_of(c.pi, name), :].rearrange("p (kt n) -> p kt n", kt=8)

        def rk(c, name):
            return slot_keys(slot_of(c.pi, name))

        def fm_group(c, name, mcol, nk, rhs_fn, rhs_keys, view=None):
            bank, bkey = next_bank()
            w = view if view is not None else v8(c, name)
            NT = c.NT

            def f(e):
                for k in range(nk):
                    ins = e.matmul(bank[:, :NT], lhsT=w[:, k, mcol * 128:(mcol + 1) * 128], rhs=rhs_fn(k), start=(k == 0), stop=(k == nk - 1))
                return ins
            S.add("pe", f, reads=rk(c, name) + rhs_keys, writes=[bkey])
            return bank, bkey

        def pre_units(c):
            chains, trs = [], []
            for tt in range(c.NTT):
                st_ = {}

                def chain(tt=tt, st_=st_):
                    TS = c.TS
                    src = xs if c.sample else xp[c.tok0 + tt * 128: c.tok0 + (tt + 1) * 128, :]
                    XK = [("vg", 0), ("vg", 1)]
                    S.add("sp", lambda e: e.dma_start(out=vg[:TS, :], in_=src), writes=XK, dkey="xpre")
                    st_["hi"] = norm_chain(vg[:TS, :], XK, TS)

                def trp(tt=tt, st_=st_):
                    norm_tr(st_["hi"], tt, c.TS, gmT, "gmT", hTa, "hTa")
                chains.append(chain)
                trs.append(trp)
            return chains, trs

        def sblock_unit(c):
            HK = [("hTa", tt) for tt in range(c.NTT)]
            for m in range(4):
                bank, bkey = fm_group(c, "s", m, 8, lambda k: hTa[:, k, :c.NT], HK)
                S.add("act", lambda e, m=m, bank=bank: e.copy(out=sT[:, m, :c.NT], in_=bank[:, :c.NT]), reads=[bkey], writes=[("sT", m)])
            done_item(c.pi, "s")

        def ssm_units(c):
            ssm = []
            if c.sample:
                ssm.append(lambda: ssm_sample_unit(c))
                return ssm
            unitsA, unitsB, unitsB2, unitsC = [], [], [], []
            for tt in range(c.NTT):
                for m in range(4):
                    idx = tt * 4 + m
                    st_ = idx % 2
                    T6 = tS[6 * st_:6 * st_ + 6]
                    K6 = ["tS%d" % (6 * st_ + i) for i in range(6)]
                    cs = cosT[:, 4 * m:4 * m + 4, :]
                    sn = sinT[:, 4 * m:4 * m + 4, :]

                    def unit_a(tt=tt, m=m, T6=T6, K6=K6, cs=cs, sn=sn):
                        def f(e):
                            for ql in range(4):
                                for part in range(2):
                                    bank = bRe if part == 0 else bIm
                                    ins = e.matmul(bank[:, ql * 128:(ql + 1) * 128], lhsT=lhsT_BU[:, m, ql, part, :], rhs=sT[:, m, tt * 128:(tt + 1) * 128], start=True, stop=True)
                            return ins
                        S.add("pe", f, reads=LBU_KEYS + [("sT", m)], writes=["bRe", "bIm"])
                        re3 = bRe[:].rearrange("p (q t) -> p q t", q=4)
                        im3 = bIm[:].rearrange("p (q t) -> p q t", q=4)
                        tA, tB, tC, tD, tE, tF = T6
                        S.add("dve", lambda e: e.tensor_tensor(out=tD, in0=re3, in1=sn, op=ALU.mult), reads=["bRe", "sinT"], writes=[K6[3]])
                        S.add("dve", lambda e: e.tensor_tensor(out=tA, in0=re3, in1=cs, op=ALU.mult), reads=["bRe", "cosT"], writes=[K6[0]])
                        S.add("dve", lambda e: e.tensor_tensor(out=tB, in0=im3, in1=sn, op=ALU.mult), reads=["bIm", "sinT"], writes=[K6[1]])
                        S.add("dve", lambda e: e.tensor_tensor(out=tC, in0=im3, in1=cs, op=ALU.mult), reads=["bIm", "cosT"], writes=[K6[2]])
                        S.add("dve", lambda e: e.tensor_tensor(out=tA, in0=tA, in1=tB, op=ALU.add), reads=[K6[0], K6[1]], writes=[K6[0]])
                        S.add("dve", lambda e: e.tensor_tensor(out=tC, in0=tC, in1=tD, op=ALU.subtract), reads=[K6[2], K6[3]], writes=[K6[2]])

                    def unit_b(tt=tt, m=m, T6=T6, K6=K6, cs=cs, sn=sn, idx=idx, st_=st_):
                        tA, tB, tC, tD, tE, tF = T6

                        def fs(e):
                            for ql in range(4):
                                q = 4 * m + ql
                                e.tensor_tensor_scan(out=tB[:, ql, :], data0=rho_s[:, q:q + 1].broadcast_to([128, 128]), data1=tA[:, ql, :],
                                                     initial=hst[0][:, q:q + 1], op0=ALU.mult, op1=ALU.add)
                                ins = e.tensor_tensor_scan(out=tD[:, ql, :], data0=rho_s[:, q:q + 1].broadcast_to([128, 128]), data1=tC[:, ql, :],
                                                           initial=hst[1][:, q:q + 1], op0=ALU.mult, op1=ALU.add)
                            return ins
                        S.add("dve", fs, reads=[K6[0], K6[2], "rho_s", ("hst", 0, m), ("hst", 1, m)], writes=[K6[1], K6[3]])

                    def unit_b2(tt=tt, m=m, T6=T6, K6=K6, cs=cs, sn=sn, idx=idx, st_=st_):
                        tA, tB, tC, tD, tE, tF = T6
                        def bview(i):
                            v = tSall[:, 6 * st_ + i, :].bitcast(BF16)
                            return (v[:, 0:512].rearrange("p (a b) -> p a b", a=4), v[:, 512:1024].rearrange("p (a b) -> p a b", a=4))
                        rb_re, rb_im = bview(4)
                        u1, u2 = bview(5)
                        u3, u4 = bview(0)
                        cb = cosB[:, 4 * m:4 * m + 4, :]
                        sbq = sinB[:, 4 * m:4 * m + 4, :]
                        S.add("act", lambda e: e.copy(out=rb_re, in_=tB), reads=[K6[1]], writes=[K6[4]])
                        S.add("act", lambda e: e.copy(out=rb_im, in_=tD), reads=[K6[3], K6[4]], writes=[K6[4]])
                        p = sst[:, st_, :]
                        pk = ("sst", st_)
                        S.add("pool", lambda e: e.tensor_tensor(out=p[:, 0:4], in0=tB[:, :, 127], in1=cosT[:, 4 * m:4 * m + 4, 127], op=ALU.mult), reads=[K6[1], "cosT"], writes=[pk])
                        S.add("pool", lambda e: e.tensor_tensor(out=p[:, 4:8], in0=tD[:, :, 127], in1=sinT[:, 4 * m:4 * m + 4, 127], op=ALU.mult), reads=[K6[3], "sinT", pk], writes=[pk])
                        S.add("pool", lambda e: e.tensor_tensor(out=p[:, 8:12], in0=tB[:, :, 127], in1=sinT[:, 4 * m:4 * m + 4, 127], op=ALU.mult), reads=[K6[1], "sinT", pk], writes=[pk])
                        S.add("pool", lambda e: e.tensor_tensor(out=p[:, 12:16], in0=tD[:, :, 127], in1=cosT[:, 4 * m:4 * m + 4, 127], op=ALU.mult), reads=[K6[3], "cosT", pk], writes=[pk])
                        S.add("pool", lambda e: e.tensor_tensor(out=hst[0][:, 4 * m:4 * m + 4], in0=p[:, 0:4], in1=p[:, 4:8], op=ALU.subtract), reads=[pk], writes=[("hst", 0, m)])
                        S.add("pool", lambda e: e.tensor_tensor(out=hst[1][:, 4 * m:4 * m + 4], in0=p[:, 8:12], in1=p[:, 12:16], op=ALU.add), reads=[pk], writes=[("hst", 1, m)])
                        S.add("dve", lambda e: e.tensor_tensor(out=u1, in0=rb_re, in1=cb, op=ALU.mult), reads=[K6[4], "cosB"], writes=[K6[5]])
                        S.add("dve", lambda e: e.tensor_tensor(out=u2, in0=rb_im, in1=sbq, op=ALU.mult), reads=[K6[4], "sinB", K6[5]], writes=[K6[5]])
                        S.add("dve", lambda e: e.tensor_tensor(out=u3, in0=rb_re, in1=sbq, op=ALU.mult), reads=[K6[4], "sinB"], writes=[K6[0]])
                        S.add("dve", lambda e: e.tensor_tensor(out=u4, in0=rb_im, in1=cb, op=ALU.mult), reads=[K6[4], "cosB", K6[0]], writes=[K6[0]])
                        hb = h_bf[idx % 2]
                        hk = ("hbf", idx % 2)
                        S.add("dve", lambda e: e.tensor_tensor(out=hb[:, 0, :, :], in0=u1, in1=u2, op=ALU.subtract), reads=[K6[5]], writes=[(hk, 0)])
                        S.add("dve", lambda e: e.tensor_tensor(out=hb[:, 1, :, :], in0=u3, in1=u4, op=ALU.add), reads=[K6[0]], writes=[(hk, 1)])

                    def unit_cp(tt=tt, m=m, idx=idx):
                        hb = h_bf[idx % 2]
                        hk = ("hbf", idx % 2)

                        def fc(e):
                            o = bY[:, m * 128:(m + 1) * 128]
                            first = True
                            for ql in range(4):
                                for part in range(2):
                                    e.matmul(o, lhsT=lhsT_C[:, 4 * m + ql, part, :], rhs=hb[:, part, ql, :], start=first, stop=False)
                                    first = False
                            return e.matmul(o, lhsT=diagD[:, m, :], rhs=sT[:, m, tt * 128:(tt + 1) * 128], start=False, stop=True)
                        S.add("pe", fc, reads=LC_KEYS + DD_KEYS + [(hk, 0), (hk, 1), ("sT", m)], writes=["bY"])
                        if m == 3:
                            S.add("act", lambda e: e.copy(out=yT[:, :, tt * 128:(tt + 1) * 128], in_=bY[:].rearrange("p (m t) -> p m t", m=4)),
                                  reads=["bY"], writes=[("yT", tt)])
                    unitsA.append(unit_a)
                    unitsB.append(unit_b)
                    unitsB2.append(unit_b2)
                    unitsC.append(unit_cp)
            nq = len(unitsA)
            for i in range(nq + 2):
                def step(i=i):
                    if 1 <= i <= nq:
                        unitsB[i - 1]()
                    if i < nq:
                        unitsA[i]()
                    if 1 <= i <= nq:
                        unitsB2[i - 1]()
                    if i >= 2:
                        unitsC[i - 2]()
                ssm.append(step)
            return ssm

        def ssm_sample_unit(c):
            VGK = [("vg", 0), ("vg", 1)]
            for part, src in enumerate([h0re, h0im]):
                bank = bRe if part == 0 else bIm
                bkey = "bRe" if part == 0 else "bIm"
                for half in range(2):
                    S.add("sp", lambda e, src=src, half=half: e.dma_start(out=vg[:16, :], in_=src[:, half * 1024:(half + 1) * 1024]),
                          writes=VGK, dkey="h0ld")

                    def f(e, half=half, bank=bank):
                        for q8 in range(8):
                            q = half * 8 + q8
                            ins = e.transpose(out=bank[:, q * 16:(q + 1) * 16], in_=vg[:16, q8 * 128:(q8 + 1) * 128], identity=ident_f[:16, :16])
                        return ins
                    S.add("pe", f, reads=VGK + ["ident_f"], writes=[bkey])
                S.add("act", lambda e, part=part, bank=bank: e.copy(out=h0T[part], in_=bank[:, 0:256].rearrange("p (q t) -> p q t", q=16)),
                      reads=[bkey], writes=["tS%d" % (6 + part)])

            def f(e):
                for q in range(16):
                    for part in range(2):
                        bank = bRe if part == 0 else bIm
                        ins = e.matmul(bank[:, q * 16:(q + 1) * 16], lhsT=lhsT_BU[:, q // 4, q % 4, part, :], rhs=sT[:, q // 4, :16], start=True, stop=True)
                return ins
            S.add("pe", f, reads=LBU_KEYS + [("sT", m) for m in range(4)], writes=["bRe", "bIm"])
            lre = lb_re[:, :].unsqueeze(2).broadcast_to([128, 16, 16])
            lim = lb_im[:, :].unsqueeze(2).broadcast_to([128, 16, 16])
            a1 = tSall[:, 0, 0:256].rearrange("p (q t) -> p q t", q=16)
            a2 = tSall[:, 1, 0:256].rearrange("p (q t) -> p q t", q=16)
            bre3 = bRe[:, 0:256].rearrange("p (q t) -> p q t", q=16)
            bim3 = bIm[:, 0:256].rearrange("p (q t) -> p q t", q=16)
            S.add("dve", lambda e: e.tensor_tensor(out=a1, in0=h0T[0], in1=lre, op=ALU.mult), reads=["tS6", "lb_re"], writes=["tS0"])
            S.add("dve", lambda e: e.tensor_tensor(out=a2, in0=h0T[1], in1=lim, op=ALU.mult), reads=["tS7", "lb_im"], writes=["tS1"])
            S.add("dve", lambda e: e.tensor_tensor(out=a1, in0=a1, in1=a2, op=ALU.subtract), reads=["tS0", "tS1"], writes=["tS0"])
            S.add("dve", lambda e: e.tensor_tensor(out=hs[0], in0=bre3, in1=a1, op=ALU.add), reads=["tS0", "bRe"], writes=["tS8"])
            S.add("dve", lambda e: e.tensor_tensor(out=a1, in0=h0T[1], in1=lre, op=ALU.mult), reads=["tS7", "lb_re"], writes=["tS0"])
            S.add("dve", lambda e: e.tensor_tensor(out=a2, in0=h0T[0], in1=lim, op=ALU.mult), reads=["tS6", "lb_im"], writes=["tS1"])
            S.add("dve", lambda e: e.tensor_tensor(out=a1, in0=a1, in1=a2, op=ALU.add), reads=["tS0", "tS1"], writes=["tS0"])
            S.add("dve", lambda e: e.tensor_tensor(out=hs[1], in0=bim3, in1=a1, op=ALU.add), reads=["tS0", "bIm"], writes=["tS9"])
            for part in range(2):
                S.add("act", lambda e, part=part: e.copy(out=hs_bf[:, :, part, :], in_=hs[part]), reads=["tS%d" % (8 + part)], writes=[("tmpb", 1)])

            def fc(e):
                for m in range(4):
                    o = bY[:, m * 128:m * 128 + 16]
                    first = True
                    for ql in range(4):
                        for part in range(2):
                            e.matmul(o, lhsT=lhsT_C[:, 4 * m + ql, part, :], rhs=hs_bf[:, 4 * m + ql, part, :], start=first, stop=False)
                            first = False
                    ins = e.matmul(o, lhsT=diagD[:, m, :], rhs=sT[:, m, :16], start=False, stop=True)
                return ins
            S.add("pe", fc, reads=LC_KEYS + DD_KEYS + [("tmpb", 1)] + [("sT", m) for m in range(4)], writes=["bY"])
            S.add("act", lambda e: e.copy(out=yT[:, :, 0:16], in_=bY[:].rearrange("p (m t) -> p m t", m=4)[:, :, 0:16]),
                  reads=["bY"], writes=[("yT", 0)])
            for part in range(2):
                dsto = sre_s if part == 0 else sim_s
                for grp in range(4):
                    bank, bkey = next_bank()

                    def ft(e, part=part, grp=grp, bank=bank):
                        for ql in range(4):
                            ins = e.transpose(out=bank[:16, ql * 128:(ql + 1) * 128], in_=hs[part][:, 4 * grp + ql, :], identity=ident_f[:, :])
                        return ins
                    S.add("pe", ft, reads=["tS%d" % (8 + part), "ident_f"], writes=[bkey])
                    S.add("act", lambda e, bank=bank, grp=grp: e.copy(out=vg[:16, (grp % 2) * 512:(grp % 2) * 512 + 512], in_=bank[:16, :]),
                          reads=[bkey], writes=[("vg", grp % 2)])
                    if grp % 2 == 1:
                        S.add("sp", lambda e, dsto=dsto, grp=grp: e.dma_start(out=dsto[:, (grp // 2) * 1024:(grp // 2 + 1) * 1024], in_=vg[:16, :]),
                              reads=VGK, dkey=("sout", part, grp // 2))

        def run_pass(c, nxt, own=None):
            NT, TS, NTT, sample = c.NT, c.TS, c.NTT, c.sample
            own = own or []
            own_i = [0]
            own_prog = [0]
            NOWN = 8 + 2 * NTT + 8 + 8 + 8

            def run_own(flush=False):
                own_prog[0] += 1
                tgt = len(own) if flush else (len(own) * own_prog[0] + NOWN - 1) // NOWN
                while own_i[0] < min(tgt, len(own)):
                    own[own_i[0]]()
                    own_i[0] += 1
            HK = [("hTa", tt) for tt in range(NTT)]
            hrhs = lambda k: hTa[:, k, :NT]
            for tt in range(NTT):
                src = xs if sample else xp[c.tok0 + tt * 128: c.tok0 + (tt + 1) * 128, :]
                S.add("sp", lambda e, tt=tt, src=src: e.dma_start(out=x_sb[tt][:TS, :], in_=src), writes=[("x", tt)], dkey=("x", tt))
            for m in range(8):
                name = "u%d" % (m // 4)
                bank, bkey = fm_group(c, name, m % 4, 8, hrhs, HK)
                S.add("act", lambda e, m=m, bank=bank: e.activation(out=big[:, m, :NT], in_=bank[:, :NT], func=AF.Gelu_apprx_tanh),
                      reads=[bkey], writes=[("big", m)])
                if m % 4 == 3:
                    done_item(c.pi, name)
                run_own()

            def v_unit(tt):
                vl = v_ln[tt % 2]
                vk = ("vln", tt % 2)
                for hh in range(2):
                    bank, bkey = next_bank()
                    w = v8(c, "v%d" % hh)

                    def f(e, w=w, bank=bank):
                        for k in range(8):
                            ins = e.matmul(bank[:TS, :], lhsT=hTa[:, k, tt * 128:tt * 128 + TS], rhs=w[:, k, :], start=(k == 0), stop=(k == 7))
                        return ins
                    S.add("pe", f, reads=rk(c, "v%d" % hh) + [("hTa", tt)], writes=[bkey])
                    S.add("act", lambda e, hh=hh, bank=bank: e.activation(out=vg[:TS, hh * 512:(hh + 1) * 512], in_=bank[:TS, :], func=AF.Gelu_apprx_tanh),
                          reads=[bkey], writes=[("vg", hh)])
                stt, sk = next_stat()
                S.add("dve", lambda e: e.bn_stats(out=stt[:TS, 0:6], in_=vg[:TS, 0:512]), reads=[("vg", 0)], writes=[sk])
                S.add("dve", lambda e: e.bn_stats(out=stt[:TS, 6:12], in_=vg[:TS, 512:1024]), reads=[("vg", 1)], writes=[sk])
                S.add("dve", lambda e: e.bn_aggr(out=stt[:TS, 12:14], in_=stt[:TS, 0:12]), reads=[sk], writes=[sk])
                S.add("dve", lambda e: e.tensor_scalar(out=stt[:TS, 14:15], in0=stt[:TS, 13:14], scalar1=EPS, scalar2=None, op0=ALU.add), reads=[sk], writes=[sk])
                S.add("act", lambda e: e.activation(out=stt[:TS, 14:15], in_=stt[:TS, 14:15], func=AF.Ln), reads=[sk], writes=[sk])
                S.add("act", lambda e: e.activation(out=stt[:TS, 14:15], in_=stt[:TS, 14:15], func=AF.Exp, scale=-0.5), reads=[sk], writes=[sk])
                VG = [("vg", 0), ("vg", 1)]
                S.add("dve", lambda e: e.tensor_scalar(out=vg[:TS, :], in0=vg[:TS, :], scalar1=stt[:TS, 12:13], scalar2=stt[:TS, 14:15], op0=ALU.subtract, op1=ALU.mult),
                      reads=[sk] + VG, writes=VG)
                S.add("dve", lambda e: e.tensor_tensor(out=vg[:TS, :], in0=vg[:TS, :], in1=lng_t[:TS, :], op=ALU.mult), reads=VG + ["lng_t"], writes=VG)
                if sample:
                    S.add("dve", lambda e: e.tensor_tensor(out=vg[:TS, :], in0=vg[:TS, :], in1=lnb_t[:TS, :], op=ALU.add), reads=VG + ["lnb_t"], writes=VG)
                    S.add("act", lambda e: e.copy(out=vl[:TS, :], in_=vg[:TS, :]), reads=VG, writes=[vk])
                    S.add("sp", lambda e: e.dma_start(out=v_s, in_=vg[:TS, :]), reads=VG, dkey="v_s")
                else:
                    S.add("dve", lambda e: e.tensor_tensor(out=vl[:TS, :], in0=vg[:TS, :], in1=lnb_t[:TS, :], op=ALU.add), reads=VG + ["lnb_t"], writes=[vk])

            def sp_unit(tt):
                vl = v_ln[tt % 2]
                vk = ("vln", tt % 2)
                for gh in range(2):
                    bank, bkey = next_bank()

                    def f(e, gh=gh, bank=bank):
                        for gl in range(4):
                            g = 4 * gh + gl
                            if sample:
                                rhs = WspT_s[:16, g, :16]
                                brhs = bsps_bf[0:1, g, :16]
                            else:
                                rhs = WspT[:, g, :]
                                brhs = bsp_bf[0:1, g, :]
                            o = bank[:, gl * 128:gl * 128 + TS]
                            e.matmul(o, lhsT=vl[:TS, g * 128:(g + 1) * 128], rhs=rhs, start=True, stop=False)
                            ins = e.matmul(o, lhsT=ones_bf[0:1, :], rhs=brhs, start=False, stop=True)
                        return ins
                    S.add("pe", f, reads=[vk, "WspT", "WspT_s", "bsp_bf", "bsps_bf", "ones_bf"], writes=[bkey])
                    GK = [("big", 4 * gh + gl) for gl in range(4)]
                    S.add("dve", lambda e, gh=gh, bank=bank: e.tensor_tensor(
                        out=big[:, 4 * gh:4 * gh + 4, tt * 128:tt * 128 + TS],
                        in0=bank[:].rearrange("p (g t) -> p g t", g=4)[:, :, :TS],
                        in1=big[:, 4 * gh:4 * gh + 4, tt * 128:tt * 128 + TS], op=ALU.mult),
                        reads=[bkey] + GK, writes=GK)

            def ga_unit(m):
                name = "ga%d" % (m // 4)
                bank, bkey = fm_group(c, name, m % 4, 8, hrhs, HK)
                S.add("act", lambda e: e.activation(out=big[:, 8 + m, :NT], in_=bank[:, :NT], func=AF.Sigmoid), reads=[bkey], writes=[("big", 8 + m)])
                run_own()

            ga_i = 0
            for tt in range(NTT):
                v_unit(tt)
                run_own()
                if tt >= 1:
                    for _ in range(2):
                        if ga_i < 4:
                            ga_unit(ga_i)
                            ga_i += 1
                    sp_unit(tt - 1)
                    run_own()
            done_item(c.pi, "v0")
            done_item(c.pi, "v1")
            while ga_i < 4:
                ga_unit(ga_i)
                ga_i += 1
            done_item(c.pi, "ga0")
            ga_unit(4)
            ga_unit(5)
            sp_unit(NTT - 1)
            ga_unit(6)
            ga_unit(7)
            done_item(c.pi, "ga1")
            GKall = [("big", k) for k in range(8)]
            for m in range(8):
                name = "wa%d" % (m // 4)
                bank, bkey = fm_group(c, name, m % 4, 8, lambda k: big[:, k, :NT], GKall)
                S.add("dve", lambda e, m=m, bank=bank: e.tensor_tensor(out=big[:, 8 + m, :NT], in0=bank[:, :NT], in1=big[:, 8 + m, :NT], op=ALU.mult),
                      reads=[bkey, ("big", 8 + m)], writes=[("big", 8 + m)])
                if m % 4 == 3:
                    done_item(c.pi, name)
                run_own()
            if nxt is not None:
                chains, trs = pre_units(nxt)
            else:
                chains, trs = [], []
            for m in range(8):
                name = "gb%d" % (m // 4)
                if m == 4 and chains:
                    chains[0]()
                bank, bkey = fm_group(c, name, m % 4, 8, hrhs, HK)
                S.add("act", lambda e, m=m, bank=bank: e.activation(out=big[:, m, :NT], in_=bank[:, :NT], func=AF.Sigmoid), reads=[bkey], writes=[("big", m)])
                if m % 4 == 3:
                    done_item(c.pi, name)
                run_own()
            run_own(flush=True)
            if len(chains) > 1:
                chains[1]()
            side = []
            side_i = [0]
            prog = [0]
            NMAIN = NTT + 11 + 2 * NTT

            def run_side():
                prog[0] += 1
                ntot = NMAIN + (40 if (nxt is not None and not nxt.sample) else 0)
                tgt = (len(side) * prog[0] + ntot - 1) // ntot
                while side_i[0] < min(tgt, len(side)):
                    side[side_i[0]]()
                    side_i[0] += 1

            YK = [("yT", tt) for tt in range(NTT)]

            def wb_unit(m):
                vb0 = ring[:, slot_of(c.pi, "wb0"), :].rearrange("p (kt n) -> p kt n", kt=4)
                vb1 = ring[:, slot_of(c.pi, "wb1"), :].rearrange("p (kt n) -> p kt n", kt=4)
                bV, kV = fm_group(c, "wb0", m, 4, lambda k: yT[:, k, :NT], YK, view=vb0)
                bG, kG = fm_group(c, "wb1", m, 4, lambda k: yT[:, k, :NT], YK, view=vb1)
                ti = next_tmp()
                S.add("act", lambda e: e.activation(out=tmp_f[ti][:, :NT], in_=bG[:, :NT], func=AF.Sigmoid), reads=[kG], writes=[("tmpf", ti)])
                S.add("dve", lambda e: e.tensor_tensor(out=tmp_b[ti][:, :NT], in0=bV[:, :NT], in1=tmp_f[ti][:, :NT], op=ALU.mult),
                      reads=[kV, ("tmpf", ti)], writes=[("tmpb", ti)])
                S.add("dve", lambda e: e.tensor_tensor(out=tmp_b[ti][:, :NT], in0=tmp_b[ti][:, :NT], in1=big[:, m, :NT], op=ALU.mult),
                      reads=[("tmpb", ti), ("big", m)], writes=[("tmpb", ti)])
                S.add("dve", lambda e: e.tensor_tensor(out=big[:, m, :NT], in0=tmp_b[ti][:, :NT], in1=big[:, 8 + m, :NT], op=ALU.add),
                      reads=[("tmpb", ti), ("big", 8 + m)], writes=[("big", m)])

            npre = len(trs)
            for m in range(8):
                if m % 2 == 0:
                    pidx = m // 2
                    if pidx < npre:
                        trs[pidx]()
                        if pidx + 2 < npre:
                            chains[pidx + 2]()
                        if pidx == npre - 1:
                            sblock_unit(nxt)
                            side.extend(ssm_units(nxt))
                wb_unit(m)
            done_item(c.pi, "wb0")
            done_item(c.pi, "wb1")
            MK = [("big", k) for k in range(8)]

            def wo_unit(tt):
                for hh in range(2):
                    bank, bkey = next_bank()
                    w = v8(c, "wo%d" % hh)

                    def f(e, w=w, bank=bank):
                        for k in range(8):
                            ins = e.matmul(bank[:TS, :], lhsT=big[:, k, tt * 128:tt * 128 + TS], rhs=w[:, k, :], start=(k == 0), stop=(k == 7))
                        return ins
                    S.add("pe", f, reads=rk(c, "wo%d" % hh) + MK, writes=[bkey])
                    S.add("dve", lambda e, hh=hh, bank=bank: e.tensor_tensor(out=x_sb[tt][:TS, hh * 512:(hh + 1) * 512], in0=bank[:TS, :],
                                                                              in1=x_sb[tt][:TS, hh * 512:(hh + 1) * 512], op=ALU.add),
                          reads=[bkey, ("x", tt)], writes=[("x", tt)])

            his = {}
            for tt in range(NTT + 2):
                if tt < NTT:
                    wo_unit(tt)
                if 2 <= tt and tt - 2 < NTT:
                    norm_tr(his[tt - 2], tt - 2, TS, gfT, "gfT", hTb, "hTb")
                if tt < NTT:
                    his[tt] = norm_chain(x_sb[tt][:TS, :], [("x", tt)], TS)
                    run_side()
            done_item(c.pi, "wo0")
            done_item(c.pi, "wo1")

            HKb = [("hTb", tt) for tt in range(NTT)]
            hrhsb = lambda k: hTb[:, k, :NT]
            for cc in range(11):
                name = "gu%d" % cc
                vgu = ring[:, slot_of(c.pi, name), :].rearrange("p (two kt n) -> p two kt n", two=2, kt=8)
                for j in range(2):
                    bG, kG = fm_group(c, name, j, 8, hrhsb, HKb, view=vgu[:, 0])
                    bU, kU = fm_group(c, name, j, 8, hrhsb, HKb, view=vgu[:, 1])
                    ti = next_tmp()
                    S.add("act", lambda e, bG=bG, ti=ti: e.activation(out=tmp_f[ti][:, :NT], in_=bG[:, :NT], func=AF.Silu), reads=[kG], writes=[("tmpf", ti)])
                    S.add("dve", lambda e, bU=bU, ti=ti, cc=cc, j=j: e.tensor_tensor(out=big[:, 2 * cc + j, :NT], in0=bU[:, :NT], in1=tmp_f[ti][:, :NT], op=ALU.mult),
                          reads=[kU, ("tmpf", ti)], writes=[("big", 2 * cc + j)])
                done_item(c.pi, name)
                run_side()
            AK = [("big", k) for k in range(22)]
            for hh in range(2):
                ws = [v8(c, "d%d" % (3 * hh + p_)) for p_ in range(3)]
                wkeys = rk(c, "d%d" % (3 * hh)) + rk(c, "d%d" % (3 * hh + 1)) + rk(c, "d%d" % (3 * hh + 2))
                for tt in range(NTT):
                    bank, bkey = next_bank()

                    def f(e, tt=tt, ws=ws, bank=bank):
                        for k in range(22):
                            ins = e.matmul(bank[:TS, :], lhsT=big[:, k, tt * 128:tt * 128 + TS], rhs=ws[k // 8][:, k % 8, :], start=(k == 0), stop=(k == 21))
                        return ins
                    S.add("pe", f, reads=wkeys + AK, writes=[bkey])
                    S.add("dve", lambda e, tt=tt, hh=hh, bank=bank: e.tensor_tensor(out=x_sb[tt][:TS, hh * 512:(hh + 1) * 512], in0=bank[:TS, :],
                                                                                     in1=x_sb[tt][:TS, hh * 512:(hh + 1) * 512], op=ALU.add),
                          reads=[bkey, ("x", tt)], writes=[("x", tt)])
                    if hh == 1:
                        xap = x_sb[tt][:TS, :]
                        rstd, sk = rms_rstd(xap, [("x", tt)], TS, next_htm())
                        S.add("act", lambda e, xap=xap, rstd=rstd: e.activation(out=xap, in_=xap, func=AF.Identity, scale=rstd),
                              reads=[("x", tt), sk], writes=[("x", tt)])
                        S.add("pool", lambda e, xap=xap: e.tensor_tensor(out=xap, in0=xap, in1=gfin_t[:TS, :], op=ALU.mult),
                              reads=[("x", tt), "gfin_t"], writes=[("x", tt)])
                        dst = ys if sample else yp[c.tok0 + tt * 128: c.tok0 + (tt + 1) * 128, :]
                        S.add("sp", lambda e, dst=dst, xap=xap: e.dma_start(out=dst, in_=xap), reads=[("x", tt)], dkey=("yo", tt))
                    run_side()
                for p_ in range(3):
                    done_item(c.pi, "d%d" % (3 * hh + p_))
            return side[side_i[0]:]

        ctxs = [Ctx(k, False, k * 512) for k in range(4)] + [Ctx(4, True, 0)]
        ch0, tr0 = pre_units(ctxs[0])
        for a_, b_ in zip(ch0, tr0):
            a_()
            b_()
        sblock_unit(ctxs[0])
        carry = ssm_units(ctxs[0])
        for pi in range(NPASS):
            carry = run_pass(ctxs[pi], ctxs[pi + 1] if pi + 1 < NPASS else None, own=carry)
        assert not carry

        for part in range(2):
            bank, bkey = next_bank()
            S.add("pe", lambda e, part=part, bank=bank: e.transpose(out=bank[:16, 0:128], in_=hst[part][:, :], identity=ident_f[:, :]),
                  reads=[("hst", part, m) for m in range(4)] + ["ident_f"], writes=[bkey])
            S.add("act", lambda e, part=part, bank=bank: e.copy(out=tmp_f[part][:16, 0:128], in_=bank[:16, 0:128]), reads=[bkey], writes=[("tmpf", part)])
            dsto = sre_p if part == 0 else sim_p
            S.add("sp", lambda e, part=part, dsto=dsto: e.dma_start(out=dsto, in_=tmp_f[part][:16, 0:128]), reads=[("tmpf", part)], dkey=("pst", part))

        S.emit()
    return nc


_CACHE = {}


def _host_layouts(inp):
    f = lambda a: np.ascontiguousarray(np.asarray(a, dtype=np.float32))
    L = {}
    L["g_mixT"] = f(inp["norm_mix_g"][0].reshape(8, 128).T)
    L["g_ffnT"] = f(inp["norm_ffn_g"][0].reshape(8, 128).T)
    L["g_fin"] = f(inp["norm_final_g"])
    L["ln_g"] = f(inp["ln_v_g"][0])
    L["ln_b"] = f(inp["ln_v_b"][0])
    L["w_in"] = f(inp["w_in"][0])
    L["w_a"] = f(inp["w_branch_a"][0])
    L["w_b"] = f(inp["w_branch_b"][0])
    L["w_out"] = f(inp["w_out"][0])
    L["w_g"] = f(inp["w_gate_ffn"][0])
    L["w_u"] = f(inp["w_up_ffn"][0])
    L["w_d"] = f(inp["w_down_ffn"][0])
    wsp = np.asarray(inp["w_spatial"][0], dtype=np.float32)
    L["wspT"] = f(wsp.transpose(2, 0, 1))
    s_idx = np.arange(128)
    L["trimask"] = f((s_idx[:, None] <= s_idx[None, :]).astype(np.float32))
    bs = np.asarray(inp["b_spatial"][0], dtype=np.float32)
    L["bsp"] = f(bs.reshape(1, 1024))
    L["wdiag"] = f(np.broadcast_to(wsp[:, 0, 0][None, :], (16, 8)))
    L["bsp_s"] = f(np.broadcast_to(bs[:, 0][:, None], (8, 16)).reshape(1, 128))
    L["ident"] = f(np.eye(128))
    lam_re = np.asarray(inp["ssm_lam_re"][0], dtype=np.float32)
    lam_im = np.asarray(inp["ssm_lam_im"][0], dtype=np.float32)
    log_dt = np.asarray(inp["ssm_log_dt"][0], dtype=np.float32)
    def st_layout(a):
        return f(a.reshape(16, 2, 64).transpose(1, 2, 0).reshape(128, 16))
    L["lamre_s"] = st_layout(lam_re)
    L["lamim_s"] = st_layout(lam_im)
    L["logdt_s"] = st_layout(np.broadcast_to(log_dt[:, None], (32, 64)))
    def b_layout_gp(a):
        x = a.reshape(4, 8, 64)
        x = np.broadcast_to(x[:, :, None, :], (4, 8, 16, 64))
        return f(x.transpose(1, 2, 0, 3).reshape(128, 256))
    L["lamre_b"] = b_layout_gp(lam_re)
    L["lamim_b"] = b_layout_gp(lam_im)
    L["logdt_b"] = b_layout_gp(np.broadcast_to(log_dt[:, None], (32, 64)))
    def b_layout_B(a):
        x = a.reshape(4, 8, 64, 16)
        return f(x.transpose(1, 3, 0, 2).reshape(128, 256))
    L["Bre_b"] = b_layout_B(np.asarray(inp["ssm_b_re"][0], dtype=np.float32))
    L["Bim_b"] = b_layout_B(np.asarray(inp["ssm_b_im"][0], dtype=np.float32))
    row = np.arange(128)
    mB = np.zeros((128, 4, 2), np.float32)
    for ql in range(4):
        for g2 in range(2):
            mB[:, ql, g2] = ((row // 16) == (2 * ql + g2))
    L["maskB"] = f(mB.reshape(128, 8))
    def c_layout(a):
        x = a.reshape(16, 2, 16, 64)
        return f(x.transpose(1, 3, 0, 2).reshape(128, 256))
    L["Cre_l"] = c_layout(np.asarray(inp["ssm_c_re"][0], dtype=np.float32))
    L["Cim_l"] = c_layout(np.asarray(inp["ssm_c_im"][0], dtype=np.float32))
    mC = np.zeros((128, 16, 4, 2), np.float32)
    for q in range(16):
        for g2 in range(2):
            mC[g2 * 64:(g2 + 1) * 64, q, q % 4, g2] = 1.0
    L["maskC"] = f(mC.reshape(128, 128))
    L["dvec"] = f(np.asarray(inp["ssm_d"][0], dtype=np.float32).reshape(4, 128).T)
    return L


def kernel(**inp):
    if "nc" not in _CACHE:
        _CACHE["nc"] = build_program()
    nc = _CACHE["nc"]
    L = _host_layouts(inp)
    xpr = np.asarray(inp["x_prompt"], dtype=np.float32)
    xsa = np.asarray(inp["x_sample"], dtype=np.float32).reshape(128, D)
    sre = np.asarray(inp["state_ssm_re"], dtype=np.float32).reshape(128, 2048)
    sim = np.asarray(inp["state_ssm_im"], dtype=np.float32).reshape(128, 2048)
    in_maps = []
    for c in range(8):
        m = dict(L)
        m["xp"] = np.ascontiguousarray(xpr[c])
        m["xs"] = np.ascontiguousarray(xsa[16 * c:16 * c + 16])
        m["h0re"] = np.ascontiguousarray(sre[16 * c:16 * c + 16])
        m["h0im"] = np.ascontiguousarray(sim[16 * c:16 * c + 16])
        in_maps.append(m)
    res = run_bass_kernel_spmd(nc, in_maps, core_ids=list(range(8)))
    R = res.results
    y_prompt = np.stack([R[c]["yp"] for c in range(8)]).astype(np.float32)
    y_sample = np.concatenate([R[c]["ys"] for c in range(8)]).reshape(128, 1, D).astype(np.float32)
    p_re = np.stack([R[c]["sre_p"].reshape(32, 64) for c in range(8)])[None].astype(np.float32)
    p_im = np.stack([R[c]["sim_p"].reshape(32, 64) for c in range(8)])[None].astype(np.float32)
    s_re = np.concatenate([R[c]["sre_s"] for c in range(8)]).reshape(1, 128, 32, 64).astype(np.float32)
    s_im = np.concatenate([R[c]["sim_s"] for c in range(8)]).reshape(1, 128, 32, 64).astype(np.float32)
    v_s = np.concatenate([R[c]["v_s"] for c in range(8)]).reshape(1, 128, 1, D).astype(np.float32)
    return (y_prompt, y_sample, p_re, p_im, s_re, s_im, v_s)
```
